# Optimizing a Trainium2 kernel written in Bass

```python
import math
import jax, jax.numpy as jnp
from jax import lax
import numpy as np

D_MODEL = 1024
BATCH = 8
SEQ = 8192
DEPTH = 1
DEC_BATCH = 32
DEC_SEQ = 2048
PAST_LEN = 128

HEAD_DIM = 64
N_HEADS_A = 8
KV_HEADS_A = 2
N_HEADS_B = 8
KV_HEADS_B = 2
N_HEADS_C = 4
HEAD_DIM_C = 128
MEM_LEN = 256
WIDTH_A = N_HEADS_A * HEAD_DIM
WIDTH_B = N_HEADS_B * HEAD_DIM
WIDTH_C = N_HEADS_C * HEAD_DIM_C
D_FF = 2816
BLOCK = 128
WINDOW = 128
GRID_W = 64
ROPE_THETA = 10000.0
N_BUCKETS = 32
MAX_DISTANCE = 128
EPS = 1e-6
NEG_INF = -1e30
SPLIT_WIDTHS = (WIDTH_A, KV_HEADS_A * HEAD_DIM, KV_HEADS_A * HEAD_DIM,
                WIDTH_B, KV_HEADS_B * HEAD_DIM, KV_HEADS_B * HEAD_DIM,
                WIDTH_C, D_MODEL, D_MODEL, D_MODEL)
W_IN_COLS = sum(SPLIT_WIDTHS)

kernel_name = "hybrid_gated_encoder_block"


def _rmsnorm(x, g):
    xf = x.astype(jnp.float32)
    y = xf * lax.rsqrt(jnp.mean(xf * xf, axis=-1, keepdims=True) + EPS)
    return (y * g.astype(jnp.float32)).astype(x.dtype)


def _swiglu(x, w_in, w_out):
    gu = x @ w_in
    g, u = jnp.split(gu, 2, axis=-1)
    return (jax.nn.silu(g) * u) @ w_out


def _axial_rope(seq_len):
    rows = seq_len // GRID_W
    row = jnp.repeat(jnp.arange(rows, dtype=jnp.float32), GRID_W)
    col = jnp.tile(jnp.arange(GRID_W, dtype=jnp.float32), rows)
    axis_dim = HEAD_DIM // 2
    inv = ROPE_THETA ** (-jnp.arange(0, axis_dim, 2, dtype=jnp.float32) / axis_dim)
    ang = jnp.concatenate([row[:, None] * inv, col[:, None] * inv], axis=-1)
    return jnp.cos(ang), jnp.sin(ang)


def _apply_rope(x, cos, sin):
    B, S, H, D = x.shape
    xf = x.astype(jnp.float32).reshape(B, S, H, D // 2, 2)
    x0, x1 = xf[..., 0], xf[..., 1]
    c = cos[None, :, None, :]
    s = sin[None, :, None, :]
    out = jnp.stack([x0 * c - x1 * s, x0 * s + x1 * c], axis=-1)
    return out.reshape(B, S, H, D).astype(x.dtype)


def _t5_bucket(rel):
    nb = N_BUCKETS // 2
    max_exact = nb // 2
    ret = jnp.where(rel > 0, nb, 0)
    n = jnp.abs(rel)
    nf = jnp.maximum(n, 1).astype(jnp.float32)
    large = max_exact + (jnp.log(nf / max_exact) / math.log(MAX_DISTANCE / max_exact)
                         * (nb - max_exact)).astype(jnp.int32)
    large = jnp.minimum(large, nb - 1)
    return ret + jnp.where(n < max_exact, n, large)


def _global_attn(q, k, v):
    B, S, H, hd = q.shape
    kvh = k.shape[2]
    G = H // kvh
    nblk = S // BLOCK
    scale = hd ** -0.5
    qb = q.reshape(B, nblk, BLOCK, kvh, G, hd).transpose(1, 0, 2, 3, 4, 5)

    def one(qblk):
        s = jnp.einsum('bqkgd,bskd->bkgqs', qblk, k).astype(jnp.float32) * scale
        p = jax.nn.softmax(s, axis=-1)
        return jnp.einsum('bkgqs,bskd->bqkgd', p.astype(v.dtype), v)

    ob = lax.map(one, qb)
    return ob.transpose(1, 0, 2, 3, 4, 5).reshape(B, S, H * hd)


def _window_attn(q, k, v, bias, sink):
    B, S, H, hd = q.shape
    kvh = k.shape[2]
    G = H // kvh
    nblk = S // BLOCK
    scale = hd ** -0.5
    pad = ((0, 0), (BLOCK, BLOCK), (0, 0), (0, 0))
    kp = jnp.pad(k, pad).reshape(B, nblk + 2, BLOCK, kvh, hd)
    vp = jnp.pad(v, pad).reshape(B, nblk + 2, BLOCK, kvh, hd)
    kw = jnp.concatenate([kp[:, :-2], kp[:, 1:-1], kp[:, 2:]], axis=2)
    vw = jnp.concatenate([vp[:, :-2], vp[:, 1:-1], vp[:, 2:]], axis=2)
    qb = q.reshape(B, nblk, BLOCK, kvh, G, hd)
    s = jnp.einsum('bnqkgd,bnskd->bnkgqs', qb, kw).astype(jnp.float32) * scale
    s = s + bias.astype(jnp.float32).reshape(kvh, G, BLOCK, 3 * BLOCK)
    il = jnp.arange(BLOCK)[:, None]
    jl = jnp.arange(3 * BLOCK)[None, :]
    rel = jl - BLOCK - il
    kpos = (jnp.arange(nblk) * BLOCK - BLOCK)[:, None, None] + jl[None]
    mask = (jnp.abs(rel) <= WINDOW)[None] & (kpos >= 0) & (kpos < S)
    s = jnp.where(mask[None, :, None, None], s, NEG_INF)
    sk = sink.astype(jnp.float32).reshape(kvh, G)[None, None, :, :, None]
    m = jnp.maximum(jnp.max(s, axis=-1), sk)
    p = jnp.exp(s - m[..., None])
    p = p / (jnp.sum(p, axis=-1, keepdims=True) + jnp.exp(sk - m)[..., None])
    o = jnp.einsum('bnkgqs,bnskd->bnqkgd', p.astype(v.dtype), vw)
    return o.reshape(B, S, H * hd)


def _cross_attn(q, k, v):
    B, S, H, hd = q.shape
    s = jnp.einsum('bshd,bmhd->bhsm', q, k).astype(jnp.float32) * (hd ** -0.5)
    p = jax.nn.softmax(s, axis=-1)
    return jnp.einsum('bhsm,bmhd->bshd', p.astype(v.dtype), v).reshape(B, S, H * hd)


def _trunk(x, mem, rel_bias, norm_ffn1, ffn1_w_in, ffn1_w_out, norm_mix, w_in,
           q_norm_a, k_norm_a, sink_b, norm_mem, w_mem_kv, w_br_a, w_br_b, w_br_c,
           w_out, norm_ffn2, ffn2_w_in, ffn2_w_out, norm_final):
    B, S, _ = x.shape
    cos, sin = _axial_rope(S)
    rel = jnp.arange(3 * BLOCK)[None, :] - BLOCK - jnp.arange(BLOCK)[:, None]
    bias_b = jnp.transpose(rel_bias[_t5_bucket(rel)], (2, 0, 1))
    offsets = list(np.cumsum(SPLIT_WIDTHS)[:-1])
    h = x
    for l in range(DEPTH):
        h = h + 0.5 * _swiglu(_rmsnorm(h, norm_ffn1[l]), ffn1_w_in[l], ffn1_w_out[l])
        n = _rmsnorm(h, norm_mix[l])
        proj = n @ w_in[l]
        qa, ka, va, qb, kb, vb, qc, ga, gb, gc = jnp.split(proj, offsets, axis=-1)
        qa = _apply_rope(_rmsnorm(qa.reshape(B, S, N_HEADS_A, HEAD_DIM), q_norm_a[l]), cos, sin)
        ka = _apply_rope(_rmsnorm(ka.reshape(B, S, KV_HEADS_A, HEAD_DIM), k_norm_a[l]), cos, sin)
        ya = _global_attn(qa, ka, va.reshape(B, S, KV_HEADS_A, HEAD_DIM))
        yb = _window_attn(qb.reshape(B, S, N_HEADS_B, HEAD_DIM),
                          kb.reshape(B, S, KV_HEADS_B, HEAD_DIM),
                          vb.reshape(B, S, KV_HEADS_B, HEAD_DIM), bias_b, sink_b[l])
        kvm = _rmsnorm(mem, norm_mem[l]) @ w_mem_kv[l]
        kc, vc = jnp.split(kvm, 2, axis=-1)
        M = mem.shape[1]
        yc = _cross_attn(qc.reshape(B, S, N_HEADS_C, HEAD_DIM_C),
                         kc.reshape(B, M, N_HEADS_C, HEAD_DIM_C),
                         vc.reshape(B, M, N_HEADS_C, HEAD_DIM_C))
        merged = (jax.nn.sigmoid(ga) * (ya @ w_br_a[l])
                  + jax.nn.sigmoid(gb) * (yb @ w_br_b[l])
                  + jax.nn.sigmoid(gc) * (yc @ w_br_c[l]))
        h = h + merged @ w_out[l]
        h = h + 0.5 * _swiglu(_rmsnorm(h, norm_ffn2[l]), ffn2_w_in[l], ffn2_w_out[l])
    return _rmsnorm(h, norm_final)


def setup_inputs(seed: int = 0) -> dict:
    key = jax.random.key(seed)
    ks = jax.random.split(key, 32)
    f32 = jnp.float32

    def nrm(k, shape, scale):
        return jax.random.normal(k, shape, f32) * scale

    def gain(k, shape):
        return 1.0 + 0.05 * jax.random.normal(k, shape, f32)

    L, D = DEPTH, D_MODEL
    return {
        "x_prompt": nrm(ks[0], (BATCH, SEQ, D), 1.0),
        "x_sample": nrm(ks[1], (DEC_BATCH, DEC_SEQ, D), 1.0),
        "mem_prompt": nrm(ks[2], (BATCH, MEM_LEN, D), 1.0),
        "mem_sample": nrm(ks[3], (DEC_BATCH, MEM_LEN, D), 1.0),
        "rel_bias": nrm(ks[4], (N_BUCKETS, N_HEADS_B), 0.5),
        "norm_ffn1": gain(ks[5], (L, D)),
        "ffn1_w_in": nrm(ks[6], (L, D, 2 * D_FF), D ** -0.5),
        "ffn1_w_out": nrm(ks[7], (L, D_FF, D), D_FF ** -0.5),
        "norm_mix": gain(ks[8], (L, D)),
        "w_in": nrm(ks[9], (L, D, W_IN_COLS), D ** -0.5),
        "q_norm_a": gain(ks[10], (L, HEAD_DIM)),
        "k_norm_a": gain(ks[11], (L, HEAD_DIM)),
        "sink_b": nrm(ks[12], (L, N_HEADS_B), 0.5),
        "norm_mem": gain(ks[13], (L, D)),
        "w_mem_kv": nrm(ks[14], (L, D, 2 * WIDTH_C), D ** -0.5),
        "w_br_a": nrm(ks[15], (L, WIDTH_A, D), WIDTH_A ** -0.5),
        "w_br_b": nrm(ks[16], (L, WIDTH_B, D), WIDTH_B ** -0.5),
        "w_br_c": nrm(ks[17], (L, WIDTH_C, D), WIDTH_C ** -0.5),
        "w_out": nrm(ks[18], (L, D, D), D ** -0.5),
        "norm_ffn2": gain(ks[19], (L, D)),
        "ffn2_w_in": nrm(ks[20], (L, D, 2 * D_FF), D ** -0.5),
        "ffn2_w_out": nrm(ks[21], (L, D_FF, D), D_FF ** -0.5),
        "norm_final": gain(ks[22], (D,)),
    }


def reference(x_prompt, x_sample, mem_prompt, mem_sample, rel_bias, norm_ffn1, ffn1_w_in,
              ffn1_w_out, norm_mix, w_in, q_norm_a, k_norm_a, sink_b, norm_mem, w_mem_kv,
              w_br_a, w_br_b, w_br_c, w_out, norm_ffn2, ffn2_w_in, ffn2_w_out, norm_final):
    y_prompt = _trunk(x_prompt, mem_prompt, rel_bias, norm_ffn1, ffn1_w_in, ffn1_w_out,
                      norm_mix, w_in, q_norm_a, k_norm_a, sink_b, norm_mem, w_mem_kv,
                      w_br_a, w_br_b, w_br_c, w_out, norm_ffn2, ffn2_w_in, ffn2_w_out,
                      norm_final)
    y_sample = _trunk(x_sample, mem_sample, rel_bias, norm_ffn1, ffn1_w_in, ffn1_w_out,
                      norm_mix, w_in, q_norm_a, k_norm_a, sink_b, norm_mem, w_mem_kv,
                      w_br_a, w_br_b, w_br_c, w_out, norm_ffn2, ffn2_w_in, ffn2_w_out,
                      norm_final)
    return (y_prompt, y_sample)
```

```python
import math
import contextlib
import numpy as np
import concourse.bass as bass
import concourse.mybir as mybir
from concourse.bass_utils import run_bass_kernel_spmd

F32 = mybir.dt.float32
BF16 = mybir.dt.bfloat16
AF = mybir.ActivationFunctionType
ALU = mybir.AluOpType
AX = mybir.AxisListType

D = 1024
DFF = 2816
NJ = 22
KC = 8
GT = 512
EPS = 1e-6
NEG = -30000.0
MEM = 256
COMPUTE = ("pe", "act", "dve", "pool")


class Trk:
    __slots__ = ("name", "w", "r")

    def __init__(self, name=""):
        self.name = name
        self.w = None
        self.r = []


class Op:
    __slots__ = ("eng", "fn", "waits", "inc", "idx", "dma_sem", "dma_val", "is_dma")

    def __init__(self, eng, fn, is_dma=False):
        self.eng = eng
        self.fn = fn
        self.waits = []
        self.inc = False
        self.idx = -1
        self.is_dma = is_dma
        self.dma_sem = None
        self.dma_val = 0


class FW:
    def __init__(self, nc, stack):
        self.nc = nc
        self.stack = stack
        self.ops = {e: [] for e in ("pe", "act", "dve", "pool", "sp")}
        self.sems = {}
        for e in COMPUTE:
            self.sems[e] = stack.enter_context(nc.semaphore("sem_" + e))
        self.waited = {e: {s: -1 for s in COMPUTE} for e in self.ops}
        self.waited_dma = {e: {} for e in self.ops}
        self.n_dma_sem = 0

    def sbuf(self, name, shape, dt):
        return self.stack.enter_context(self.nc.sbuf_tensor(name, list(shape), dt))

    def psum(self, name, shape, dt):
        return self.stack.enter_context(self.nc.psum_tensor(name, list(shape), dt))

    def new_dma_sem(self):
        self.n_dma_sem += 1
        h = self.stack.enter_context(self.nc.semaphore(f"dsem{self.n_dma_sem}"))
        return [h, 0]

    def _deps(self, op, reads, writes):
        e = op.eng
        best = {}
        bestd = {}
        def consider(d, kind):
            if d is None or d is op:
                return
            if d.is_dma:
                sid = id(d.dma_sem)
                if sid not in bestd or d.dma_val > bestd[sid].dma_val:
                    bestd[sid] = d
                return
            if d.eng == e and not op.is_dma:
                if e == "pe" or kind != "raw":
                    return
            if d.eng not in best or d.idx > best[d.eng].idx:
                best[d.eng] = d
        for t in reads:
            consider(t.w, "raw")
        for t in writes:
            consider(t.w, "waw")
            for r in t.r:
                consider(r, "war")
        for sid, d in bestd.items():
            if d.dma_val > self.waited_dma[e].get(sid, 0):
                self.waited_dma[e][sid] = d.dma_val
                op.waits.append(d)
        for se, d in best.items():
            if d.idx > self.waited[e][se]:
                self.waited[e][se] = d.idx
                d.inc = True
                op.waits.append(d)
        for t in writes:
            t.w = op
            t.r = []
        for t in reads:
            if t.w is not op:
                t.r.append(op)

    def emit(self, eng, fn, reads=(), writes=()):
        op = Op(eng, fn)
        op.idx = len(self.ops[eng])
        self._deps(op, reads, writes)
        self.ops[eng].append(op)
        return op

    def dma(self, eng, out, in_, dsem, reads=(), writes=(), skip_deps=False, **kw):
        def fn(h):
            return h.dma_start(out=out, in_=in_, **kw)
        op = Op(eng, fn, is_dma=True)
        op.idx = len(self.ops[eng])
        if skip_deps:
            for t in writes:
                t.w = op
                t.r = []
        else:
            self._deps(op, reads, writes)
        dsem[1] += 16
        op.dma_sem = dsem
        op.dma_val = dsem[1]
        self.ops[eng].append(op)
        return op

    def finish(self, final_waits=()):
        nc = self.nc
        val = {}
        for e in COMPUTE:
            c = 0
            for op in self.ops[e]:
                if op.is_dma:
                    continue
                if op.inc:
                    c += 1
                    val[id(op)] = c
        fw = self

        def do_wait(h, d):
            if d.is_dma:
                h.wait_ge(d.dma_sem[0], d.dma_val)
            else:
                h.wait_ge(fw.sems[d.eng], val[id(d)])

        def run(e, h):
            for op in fw.ops[e]:
                for d in op.waits:
                    do_wait(h, d)
                ins = op.fn(h)
                if op.is_dma:
                    ins.then_inc(op.dma_sem[0], 16)
                elif op.inc:
                    ins.then_inc(fw.sems[e], 1)
            if e == "sp":
                for d in final_waits:
                    do_wait(h, d)

        with nc.Block() as block:
            @block.tensor
            def _(h):
                run("pe", h)

            @block.scalar
            def _(h):
                run("act", h)

            @block.vector
            def _(h):
                run("dve", h)

            @block.gpsimd
            def _(h):
                run("pool", h)

            @block.sync
            def _(h):
                run("sp", h)
        return {e: len(self.ops[e]) for e in self.ops}


def _t5_bucket_np(rel):
    rel = np.asarray(rel, dtype=np.int64)
    nb = 16
    max_exact = 8
    ret = np.where(rel > 0, nb, 0)
    n = np.abs(rel)
    nf = np.maximum(n, 1).astype(np.float32)
    large = max_exact + (np.log(nf / np.float32(max_exact)) / np.float32(math.log(128 / max_exact))
                         * np.float32(nb - max_exact)).astype(np.int32)
    large = np.minimum(large, nb - 1)
    return ret + np.where(n < max_exact, n, large)


def _host_constants(max_blocks):
    t = (np.arange(max_blocks)[None, :] * 128 + np.arange(128)[:, None]).astype(np.float32)
    row = np.floor(t / 64.0).astype(np.float32)
    col = (t - row * 64.0).astype(np.float32)
    inv = (np.float32(10000.0) ** (-np.arange(0, 32, 2, dtype=np.float32) / np.float32(32))).astype(np.float32)
    ang = np.concatenate([row[:, :, None] * inv[None, None, :], col[:, :, None] * inv[None, None, :]], axis=-1)
    ang = ang.astype(np.float32)
    cs = np.concatenate([np.cos(ang), np.sin(ang)], axis=-1).astype(np.float32)
    j = np.arange(128)[:, None, None]
    o = np.arange(3)[None, :, None]
    i = np.arange(128)[None, None, :]
    rel = o * 128 + j - 128 - i
    bucket = _t5_bucket_np(rel)
    valid = np.abs(rel) <= 128
    oh = np.zeros((33, 128, 3, 128), dtype=np.float32)
    for b in range(32):
        oh[b] = ((bucket == b) & valid).astype(np.float32)
    oh[32] = (~valid).astype(np.float32)
    ident = np.eye(128, dtype=np.float32)
    return cs, oh.reshape(33, 128, 384), ident


class Builder:
    def __init__(self, seqs, debug=False):
        self.debug = debug
        self.seqs = list(seqs)
        self.ntok = sum(seqs)
        self.nseq = len(seqs)
        self.maxblk = max(seqs) // 128
        self.pb = 0

    def mm(self, out, lhsT, rhs, start, stop, reads, wtrk):
        self.fw.emit("pe", lambda h: h.matmul(out, lhsT=lhsT, rhs=rhs, start=start, stop=stop), reads, [wtrk])

    def tr(self, out, in_, reads, wtrk):
        ident = self.ident[:]
        self.fw.emit("pe", lambda h: h.transpose(out=out, in_=in_, identity=ident), list(reads) + [self.t_ident], [wtrk])

    def act(self, out, in_, func, reads, writes, scale=1.0, bias=0.0, accum=None):
        if accum is None:
            self.fw.emit("act", lambda h: h.activation(out=out, in_=in_, func=func, bias=bias, scale=scale), reads, writes)
        else:
            self.fw.emit("act", lambda h: h.activation(out=out, in_=in_, func=func, bias=bias, scale=scale,
                                                       accum_out=accum), reads, writes)

    def tt(self, eng, out, in0, in1, op, reads, writes):
        self.fw.emit(eng, lambda h: h.tensor_tensor(out=out, in0=in0, in1=in1, op=op), reads, writes)

    def stt(self, out, in0, scalar, in1, op0, op1, reads, writes):
        self.fw.emit("dve", lambda h: h.scalar_tensor_tensor(out=out, in0=in0, scalar=scalar, in1=in1, op0=op0, op1=op1),
                     reads, writes)

    def ts(self, eng, out, in0, s1, op0, reads, writes, s2=None, op1=None):
        if op1 is None:
            self.fw.emit(eng, lambda h: h.tensor_scalar(out=out, in0=in0, scalar1=s1, scalar2=None, op0=op0), reads, writes)
        else:
            self.fw.emit(eng, lambda h: h.tensor_scalar(out=out, in0=in0, scalar1=s1, scalar2=s2, op0=op0, op1=op1),
                         reads, writes)

    def cp(self, eng, out, in_, reads, writes):
        if eng == "act":
            self.fw.emit("act", lambda h: h.copy(out=out, in_=in_), reads, writes)
        else:
            self.fw.emit(eng, lambda h: h.tensor_copy(out=out, in_=in_), reads, writes)

    def recip(self, out, in_, reads, writes):
        self.fw.emit("dve", lambda h: h.reciprocal(out=out, in_=in_), reads, writes)

    def memset(self, eng, ap, v, writes):
        self.fw.emit(eng, lambda h: h.memset(ap, v), [], writes)

    def bank(self):
        b = self.pb % 6
        self.pb += 1
        return self.psF[b], self.t_psF[b]

    def bankT(self):
        b = self.pbT % 2
        self.pbT += 1
        return self.psT[b], self.t_psT[b]

    def build(self):
        nc = bass.Bass("TRN2", target_bir_lowering=False)
        self.nc = nc
        NT = self.ntok
        dt_in = lambda n, s: nc.dram_tensor(n, list(s), F32, kind="ExternalInput").ap()
        self.x = dt_in("x", [NT, D])
        self.mem = dt_in("mem", [self.nseq * MEM, D])
        self.rel_bias = dt_in("rel_bias", [32, 8])
        self.norm_ffn1 = dt_in("norm_ffn1", [1, D])
        self.ffn1_w_in = dt_in("ffn1_w_in", [1, D, 2 * DFF])
        self.ffn1_w_out = dt_in("ffn1_w_out", [1, DFF, D])
        self.norm_mix = dt_in("norm_mix", [1, D])
        self.w_in = dt_in("w_in", [1, D, 5120])
        self.q_norm_a = dt_in("q_norm_a", [1, 64])
        self.k_norm_a = dt_in("k_norm_a", [1, 64])
        self.sink_b = dt_in("sink_b", [1, 8])
        self.norm_mem = dt_in("norm_mem", [1, D])
        self.w_mem_kv = dt_in("w_mem_kv", [1, D, D])
        self.w_br = [dt_in("w_br_a", [1, 512, D]), dt_in("w_br_b", [1, 512, D]), dt_in("w_br_c", [1, 512, D])]
        self.w_out = dt_in("w_out", [1, D, D])
        self.norm_ffn2 = dt_in("norm_ffn2", [1, D])
        self.ffn2_w_in = dt_in("ffn2_w_in", [1, D, 2 * DFF])
        self.ffn2_w_out = dt_in("ffn2_w_out", [1, DFF, D])
        self.norm_final = dt_in("norm_final", [D])
        self.c_cs = dt_in("c_cs", [128, self.maxblk, 64])
        self.c_oh = dt_in("c_oh", [33, 128, 384])
        self.c_ident = dt_in("c_ident", [128, 128])
        self.y = nc.dram_tensor("y", [NT, D], F32, kind="ExternalOutput").ap()
        if self.debug:
            self.h1s = nc.dram_tensor("h1s", [NT, D], F32, kind="ExternalOutput").ap()
            self.dbg_big = nc.dram_tensor("dbg_big", [128, 24 * GT], BF16, kind="ExternalOutput").ap()
            self.dbg_h2 = nc.dram_tensor("dbg_h2", [GT, D], F32, kind="ExternalOutput").ap()
        else:
            self.h1s = nc.dram_tensor("h1s", [NT, D], F32).ap()
        self.kbs = nc.dram_tensor("kbs", [128, NT], BF16).ap()
        self.vbs = nc.dram_tensor("vbs", [NT // 128, 128, 256], BF16).ap()
        self.wsc = {}

        with contextlib.ExitStack() as st:
            self.fw = fw = FW(nc, st)
            self.alloc()
            self.setup()
            self.ws_plan = []
            self.ws_dry = True
            self.main()
            self.ws_dry = False
            self.ws_issued = 0
            self.ws_consumed = 0
            self.pb = 0
            self.pbT = 0
            self.main()
            counts = fw.finish(final_waits=self.out_ops)
        self.counts = counts
        return nc

    def alloc(self):
        fw = self.fw
        self.xring = fw.sbuf("xring", [128, 6, D], F32)
        self.t_x = [Trk(f"x{i}") for i in range(6)]
        self.s_x = [fw.new_dma_sem() for _ in range(6)]
        self.wring = [fw.sbuf(f"wring{i}", [128, 2048], BF16) for i in range(6)]
        self.t_w = [Trk(f"w{i}") for i in range(6)]
        self.s_w = [fw.new_dma_sem() for _ in range(6)]
        self.nT = fw.sbuf("nT", [128, KC, GT], BF16)
        self.t_nT = Trk("nT")
        self.big = fw.sbuf("big", [128, 24, GT], BF16)
        self.t_big = [Trk(f"big{i}") for i in range(24)]
        self.xnb = fw.sbuf("xnb", [128, 4, D], BF16)
        self.t_xnb = [Trk(f"xnb{i}") for i in range(4)]
        self.junk = fw.sbuf("junk", [128, D], BF16)
        self.t_junk = Trk("junk")
        mb = self.maxblk
        self.KAT = fw.sbuf("KAT", [128, mb * 128], BF16)
        self.VA = fw.sbuf("VA", [128, mb, 2, 128], BF16)
        self.t_KA = [Trk(f"KA{i}") for i in range(mb // 4)]
        self.t_VA = [Trk(f"VA{i}") for i in range(mb // 4)]
        self.KBw = [fw.sbuf(f"KBw{i}", [128, 768], BF16) for i in range(2)]
        self.VBw = [fw.sbuf(f"VBw{i}", [128, 6, 256], BF16) for i in range(2)]
        self.t_KBw = [Trk() for _ in range(2)]
        self.t_VBw = [Trk() for _ in range(2)]
        self.s_KBw = [fw.new_dma_sem() for _ in range(2)]
        self.s_VBw = [fw.new_dma_sem() for _ in range(2)]
        self.kbst = fw.sbuf("kbst", [128, GT], BF16)
        self.vbst = fw.sbuf("vbst", [128, 4, 2, 128], BF16)
        self.t_kbst = Trk()
        self.t_vbst = Trk()
        self.s_kbst = fw.new_dma_sem()
        self.s_vbst = fw.new_dma_sem()
        self.KCT = fw.sbuf("KCT", [128, 4, MEM], BF16)
        self.VC = fw.sbuf("VC", [128, 2, 512], BF16)
        self.t_KC = Trk()
        self.t_VC = Trk()
        self.s_memt = fw.new_dma_sem()
        self.PT = [fw.sbuf(f"PT{i}", [128, GT], BF16) for i in range(4)]
        self.t_PT = [Trk() for _ in range(4)]
        self.ipt = 0
        self.fsc_all = fw.sbuf("fsc_all", [128, 6, GT], F32)
        self.fsc = [self.fsc_all[:, i, :] for i in range(6)]
        self.t_fsc = [Trk() for _ in range(6)]
        self.memt = self.fsc_all[:, 4:6, :].rearrange("p a t -> p (a t)")
        self.ifs = 0
        self.q16 = fw.sbuf("q16", [128, GT], BF16)
        self.t_q16 = Trk()
        self.stat = fw.sbuf("stat", [128, 64], F32)
        self.t_ss = Trk()
        self.t_rs = Trk()
        self.t_ssq = Trk()
        self.t_rsq = Trk()
        self.bias = fw.sbuf("bias", [128, 3, 8, 128], F32)
        self.t_bias = Trk()
        self.csg = [fw.sbuf(f"csg{i}", [128, 4, 64], F32) for i in range(2)]
        self.t_csg = [Trk() for _ in range(2)]
        self.s_csg = [fw.new_dma_sem() for _ in range(2)]
        self.ics = 0
        self.ident = fw.sbuf("ident", [128, 128], BF16)
        self.t_ident = Trk()
        self.ones = fw.sbuf("ones", [128, 128], BF16)
        self.t_ones = Trk()
        self.gq = fw.sbuf("gq", [128, 64], F32)
        self.gk = fw.sbuf("gk", [128, 64], F32)
        self.t_gqk = Trk()
        self.gfin = fw.sbuf("gfin", [128, D], F32)
        self.t_gfin = Trk()
        self.esk = fw.sbuf("esk", [128, 8], F32)
        self.t_esk = Trk()
        self.gains = fw.sbuf("gains", [128, 4, KC], F32)
        self.t_gains = Trk()
        self.rbb = fw.sbuf("rbb", [128, 256], F32)
        self.t_rbb = Trk()
        self.psF = [fw.psum(f"psF{i}", [128, GT], F32) for i in range(6)]
        self.t_psF = [Trk(f"psF{i}") for i in range(6)]
        self.psT = [fw.psum(f"psT{i}", [128, 2 * GT], BF16) for i in range(2)]
        self.t_psT = [Trk(f"psT{i}") for i in range(2)]
        self.pbT = 0
        self.t_h1 = [Trk() for _ in range(self.ntok // 128)]
        self.t_kbs = [Trk() for _ in range(self.ntok // GT)]
        self.t_vbs = [Trk() for _ in range(self.ntok // GT)]
        self.out_ops = []

    def nextPT(self):
        i = self.ipt % 4
        self.ipt += 1
        return self.PT[i], self.t_PT[i]

    def nextF(self):
        i = self.ifs % 6
        self.ifs += 1
        return self.fsc[i], self.t_fsc[i]

    def setup(self):
        fw = self.fw
        nc = self.nc
        sem = fw.new_dma_sem

        def ld(eng, out, in_, trk, **kw):
            fw.dma(eng, out, in_, sem(), writes=[trk], **kw)
        ld("pool", self.ident[:], self.c_ident, self.t_ident)
        ld("sp", self.gq[:], self.q_norm_a.partition_broadcast(128).rearrange("p a b -> p (a b)"), self.t_gqk)
        ld("sp", self.gk[:], self.k_norm_a.partition_broadcast(128).rearrange("p a b -> p (a b)"), self.t_gqk)
        ld("sp", self.gfin[:], self.norm_final.partition_broadcast(128), self.t_gfin)
        ld("sp", self.esk[:], self.sink_b.partition_broadcast(128).rearrange("p a b -> p (a b)"), self.t_esk)
        ld("sp", self.rbb[:], self.rel_bias.rearrange("b h -> (b h)").partition_broadcast(128), self.t_rbb)
        for gi, g in enumerate([self.norm_ffn1, self.norm_mix, self.norm_ffn2, self.norm_mem]):
            ld("sp", self.gains[:, gi, :], g[0].rearrange("(kc p) -> p kc", p=128), self.t_gains,
               allow_slow_non_contiguous=True)
        self.act(self.esk[:], self.esk[:], AF.Exp, [self.t_esk], [self.t_esk])
        self.memset("pool", self.ones[:], 1.0, [self.t_ones])
        self.memset("pool", self.VA[:, :, :, 64:128], 1.0, self.t_VA)
        self.memset("pool", self.vbst[:, :, :, 64:128], 1.0, [self.t_vbst])
        self.memset("dve", self.bias[:], 0.0, [self.t_bias])
        for b in range(33):
            buf, tb = self.nextF()
            s = sem()
            fw.dma("sp", buf[:, 0:384], self.c_oh[b], s, writes=[tb])
            for h in range(8):
                sc = NEG if b == 32 else self.rbb[:, b * 8 + h: b * 8 + h + 1]
                self.stt(self.bias[:, :, h, :], buf[:, 0:384].rearrange("p (o i) -> p o i", o=3), sc,
                         self.bias[:, :, h, :], ALU.mult, ALU.add, [tb, self.t_bias, self.t_rbb], [self.t_bias])
        self.prep_i = 0

        def wsrc(w):
            return w[0].rearrange("(kc p) c -> p kc c", p=128)
        G1, GM, G2, GME = 0, 1, 2, 3
        for which, (win, wout, gi) in enumerate([(self.ffn1_w_in, self.ffn1_w_out, G1), (self.ffn2_w_in, self.ffn2_w_out, G2)]):
            wr = wsrc(win)
            for j in range(NJ):
                self.prep_block(("w1", which, j), 2048,
                                [((0, 128), wr[:, :, j * 128:(j + 1) * 128]),
                                 ((128, 256), wr[:, :, DFF + j * 128: DFF + (j + 1) * 128])], 256, gi, 2048)
            wo = wout[0].rearrange("(j p) c -> p j c", p=128)
            for c in range(2):
                for jb in range(6):
                    nj = 4 if jb < 5 else 2
                    self.prep_block(("w2", which, c, jb), nj * 512,
                                    [((0, 512), wo[:, jb * 4: jb * 4 + nj, c * 512:(c + 1) * 512])], 512, None, 0)
        wi = wsrc(self.w_in)
        self.prep_block(("wkv", 0), 2048, [((0, 256), wi[:, :, 512:768])], 256, GM, 2048)
        self.prep_block(("wkv", 1), 2048, [((0, 256), wi[:, :, 1280:1536])], 256, GM, 2048)
        for blk in range(2):
            for nm, base in (("wqa", 0), ("wqb", 768)):
                pieces = []
                for pp in range(2):
                    p = blk * 2 + pp
                    pieces.append(((pp * 128, pp * 128 + 64), wi[:, :, base + p * 64: base + p * 64 + 64]))
                    pieces.append(((pp * 128 + 64, pp * 128 + 128), wi[:, :, base + (p + 4) * 64: base + (p + 4) * 64 + 64]))
                self.prep_block((nm, blk), 2048, pieces, 256, GM, 2048)
            self.prep_block(("wqc", blk), 2048, [((0, 256), wi[:, :, 1536 + blk * 256: 1536 + (blk + 1) * 256])], 256, GM, 2048)
        for f in range(8):
            for bi in range(3):
                key = ("wm", f, bi)
                gbase = 2048 + bi * 1024
                wb = self.w_br[bi][0]
                pieces = [((0, 128), wi[:, :, gbase + f * 128: gbase + (f + 1) * 128])]
                extra = []
                if bi < 2:
                    v = wb.rearrange("(h d) c -> d h c", d=64)
                    extra.append((0, 64, v[:, 0:4, f * 128:(f + 1) * 128]))
                    extra.append((64, 128, v[:, 4:8, f * 128:(f + 1) * 128]))
                else:
                    v = wb.rearrange("(h d) c -> d h c", d=128)
                    extra.append((0, 128, v[:, :, f * 128:(f + 1) * 128]))
                self.prep_block(key, 1536, pieces, 128, GM, 1024, extra=extra)
        wo = wsrc(self.w_out)
        for c in range(2):
            for kh in range(2):
                self.prep_block(("wo", c, kh), 2048, [((0, 512), wo[:, kh * 4:(kh + 1) * 4, c * 512:(c + 1) * 512])], 512, None, 0)
        wm = wsrc(self.w_mem_kv)
        for b in range(4):
            self.prep_block(("wmem", b), 2048, [((0, 256), wm[:, :, b * 256:(b + 1) * 256])], 256, GME, 2048)

    def prep_block(self, key, ncols, pieces, width, gain_idx, gain_cols, extra=()):
        fw = self.fw
        i = self.prep_i
        self.prep_i += 1
        sp = i % 3
        stg = self.xring[:, 2 * sp: 2 * sp + 2, :].rearrange("p a d -> p (a d)")
        tst = [self.t_x[2 * sp], self.t_x[2 * sp + 1]]
        ssem = self.s_x[2 * sp]
        ws = i % 6
        wsl = self.wring[ws]
        first = True
        for (c0, c1), src in pieces:
            K = src.shape[1]
            dst = stg[:, 0:K * width].rearrange("p (k c) -> p k c", k=K)[:, :, c0:c1]
            fw.dma("sp", dst, src, ssem, writes=tst, skip_deps=not first)
            first = False
        for (p0, p1, src) in extra:
            dst = stg[p0:p1, 1024:1536].rearrange("p (k c) -> p k c", k=4)
            fw.dma("sp", dst, src, ssem, writes=tst, skip_deps=True)
        eng = "dve" if i % 2 == 0 else "pool"
        if gain_idx is not None:
            gw = gain_cols // KC
            gap = self.gains[:, gain_idx, :].unsqueeze(2).to_broadcast([128, KC, gw])
            self.tt(eng, wsl[:, 0:gain_cols].rearrange("p (k c) -> p k c", k=KC),
                    stg[:, 0:gain_cols].rearrange("p (k c) -> p k c", k=KC), gap, ALU.mult,
                    tst + [self.t_gains], [self.t_w[ws]])
            if ncols > gain_cols:
                self.cp("act", wsl[:, gain_cols:ncols], stg[:, gain_cols:ncols], tst, [self.t_w[ws]])
        else:
            self.cp("act" if i % 2 == 0 else eng, wsl[:, 0:ncols], stg[:, 0:ncols], tst, [self.t_w[ws]])
        dr = self.nc.dram_tensor("wsc_" + "_".join(str(k) for k in key), [128, ncols], BF16).ap()
        trk = Trk()
        fw.dma("sp", dr, wsl[:, 0:ncols], self.s_w[ws], reads=[self.t_w[ws]], writes=[trk])
        self.wsc[key] = (dr, trk, ncols)

    def wget(self, key):
        if self.ws_dry:
            self.ws_plan.append(key)
            return None, None
        n = 6
        while self.ws_issued < min(len(self.ws_plan), self.ws_consumed + n - 1):
            k = self.ws_issued
            s = k % n
            ap, trk, ncols = self.wsc[self.ws_plan[k]]
            self.fw.dma("sp", self.wring[s][:, 0:ncols], ap, self.s_w[s], reads=[trk], writes=[self.t_w[s]])
            self.ws_issued += 1
        k = self.ws_consumed
        assert self.ws_plan[k] == key, (self.ws_plan[k], key)
        self.ws_consumed += 1
        return self.wring[k % n], self.t_w[k % n]

    def main(self):
        dry = self.ws_dry
        if not dry:
            self.xsched = []
            off = 0
            for s, S in enumerate(self.seqs):
                for ps in (1, 2):
                    for g in range(S // GT):
                        for i in range(4):
                            r0 = off + g * GT + i * 128
                            if ps == 1:
                                self.xsched.append((self.x[r0:r0 + 128, :], None))
                            else:
                                self.xsched.append((self.h1s[r0:r0 + 128, :], self.t_h1[r0 // 128]))
                off += S
            self.x_issued = 0
            self.gpos = 0
        off = 0
        for s, S in enumerate(self.seqs):
            self.memkv(s)
            for g in range(S // GT):
                self.pass1(s, off, S, g)
            for g in range(S // GT):
                self.pass2(s, off, S, g)
            off += S

    def xtiles(self):
        base = self.gpos * 4
        upto = min(len(self.xsched), base + 6)
        while self.x_issued < upto:
            k = self.x_issued
            src, rt = self.xsched[k]
            if rt is not None and rt.w is None:
                assert k >= base + 4
                break
            sl = k % 6
            self.fw.dma("sp", self.xring[:, sl, :], src, self.s_x[sl], reads=[] if rt is None else [rt],
                        writes=[self.t_x[sl]])
            self.x_issued += 1
        assert self.x_issued >= base + 4
        self.gpos += 1
        return [(self.xring[:, (base + i) % 6, :], self.t_x[(base + i) % 6]) for i in range(4)]

    def norm_nT(self, tiles):
        n = len(tiles)
        ss = self.stat[:, 0:n]
        rs = self.stat[:, 8:8 + n]
        self.memset("pool", ss, 0.0, [self.t_ss])
        for i, (ap, trk) in enumerate(tiles):
            self.act(self.junk[:], ap, AF.Square, [trk], [self.t_junk, self.t_ss], accum=self.stat[:, i:i + 1])
        self.act(rs, ss, AF.Sqrt, [self.t_ss], [self.t_rs], scale=1.0 / D, bias=EPS)
        self.recip(rs, rs, [self.t_rs], [self.t_rs])
        for i, (ap, trk) in enumerate(tiles):
            if i % 2 == 0:
                self.ts("dve", self.xnb[:, i, :], ap, self.stat[:, 8 + i:9 + i], ALU.mult, [trk, self.t_rs], [self.t_xnb[i]])
            else:
                self.act(self.xnb[:, i, :], ap, AF.Copy, [trk, self.t_rs], [self.t_xnb[i]], scale=self.stat[:, 8 + i:9 + i])
        for kcp in range(4):
            pt, tpt = self.bankT()
            for i in range(n):
                for kk in range(2):
                    kc = kcp * 2 + kk
                    self.tr(pt[:, kk * GT + i * 128: kk * GT + (i + 1) * 128], self.xnb[:, i, kc * 128:(kc + 1) * 128],
                            [self.t_xnb[i]], tpt)
            src = pt[:].rearrange("p (k t) -> p k t", k=2)[:, :, 0:n * 128]
            self.cp("dve" if kcp % 2 == 0 else "act", self.nT[:, 2 * kcp:2 * kcp + 2, 0:n * 128], src, [tpt], [self.t_nT])

    def ffn(self, which, tiles):
        dry = self.ws_dry
        for j in range(NJ):
            w, wt = self.wget(("w1", which, j))
            if dry:
                continue
            wv = w[:, 0:2048].rearrange("p (k c) -> p k c", k=KC)
            pg, tg = self.bank()
            pu, tu = self.bank()
            for kc in range(KC):
                self.mm(pg[:], wv[:, kc, 0:128], self.nT[:, kc, :], kc == 0, kc == KC - 1, [wt, self.t_nT], tg)
            for kc in range(KC):
                self.mm(pu[:], wv[:, kc, 128:256], self.nT[:, kc, :], kc == 0, kc == KC - 1, [wt, self.t_nT], tu)
            tb, ttb = self.nextF()
            self.act(tb[:], pg[:], AF.Tanh, [tg], [ttb], scale=0.5)
            wb, twb = self.nextF()
            self.stt(wb[:], tb[:], 1.0, pg[:], ALU.add, ALU.mult, [ttb, tg], [twb])
            self.tt("dve", self.big[:, j, :], wb[:], pu[:], ALU.mult, [twb, tu], [self.t_big[j]])
        for c in range(2):
            accs = None if dry else [self.bank() for _ in range(4)]
            for jb in range(6):
                nj = 4 if jb < 5 else 2
                w, wt = self.wget(("w2", which, c, jb))
                if dry:
                    continue
                wv = w[:, 0:nj * 512].rearrange("p (j c) -> p j c", j=nj)
                for jj in range(nj):
                    j = jb * 4 + jj
                    for i in range(4):
                        self.mm(accs[i][0][:], self.big[:, j, i * 128:(i + 1) * 128], wv[:, jj, :], j == 0, j == NJ - 1,
                                [wt, self.t_big[j]], accs[i][1])
            if dry:
                continue
            for i in range(4):
                ap, trk = tiles[i]
                self.stt(ap[:, c * 512:(c + 1) * 512], accs[i][0][:], 0.25, ap[:, c * 512:(c + 1) * 512], ALU.mult, ALU.add,
                         [accs[i][1], trk], [trk])

    def load_cs(self, g):
        i = self.ics % 2
        self.ics += 1
        self.fw.dma("sp", self.csg[i][:], self.c_cs[:, 4 * g:4 * g + 4, :], self.s_csg[i], writes=[self.t_csg[i]])
        self.cs = self.csg[i]
        self.t_cs = self.t_csg[i]

    def headnorm_rope(self, ps, tps, H, gain, blk):
        W = H * 64
        sq, tsq = self.nextF()
        self.act(sq[:, 0:W], ps[:, 0:W], AF.Square, [tps], [tsq])
        ssq = self.stat[:, 16:16 + H]
        rsq = self.stat[:, 24:24 + H]
        self.fw.emit("dve", lambda h: h.reduce_sum(out=ssq, in_=sq[:, 0:W].rearrange("p (h d) -> p h d", h=H), axis=AX.X),
                     [tsq], [self.t_ssq])
        self.act(rsq, ssq, AF.Sqrt, [self.t_ssq], [self.t_rsq], scale=1.0 / 64, bias=EPS)
        self.recip(rsq, rsq, [self.t_rsq], [self.t_rsq])
        qn, tqn = self.nextF()
        qn3 = qn[:, 0:W].rearrange("p (h d) -> p h d", h=H)
        self.tt("dve", qn3, ps[:, 0:W].rearrange("p (h d) -> p h d", h=H), rsq.unsqueeze(2).to_broadcast([128, H, 64]),
                ALU.mult, [tps, self.t_rsq], [tqn])
        self.tt("dve", qn3, qn3, gain[:].unsqueeze(1).to_broadcast([128, H, 64]), ALU.mult, [tqn, self.t_gqk], [tqn])
        v = qn[:, 0:W].rearrange("p (h i two) -> p h i two", h=H, two=2)
        x0, x1 = v[:, :, :, 0], v[:, :, :, 1]
        o = self.q16[:, 0:W].rearrange("p (h i two) -> p h i two", h=H, two=2)
        cosb = self.cs[:, blk, 0:32].unsqueeze(1).to_broadcast([128, H, 32])
        sinb = self.cs[:, blk, 32:64].unsqueeze(1).to_broadcast([128, H, 32])
        ra, tra = self.nextF()
        rb, trb = self.nextF()
        ra3 = ra[:, 0:H * 32].rearrange("p (h i) -> p h i", h=H)
        ra3b = ra[:, 256:256 + H * 32].rearrange("p (h i) -> p h i", h=H)
        rb3 = rb[:, 0:H * 32].rearrange("p (h i) -> p h i", h=H)
        rb3b = rb[:, 256:256 + H * 32].rearrange("p (h i) -> p h i", h=H)
        rd = [tqn, self.t_cs]
        self.tt("dve", ra3, x0, cosb, ALU.mult, rd, [tra])
        self.tt("dve", ra3b, x1, sinb, ALU.mult, rd, [tra])
        self.tt("dve", o[:, :, :, 0], ra3, ra3b, ALU.subtract, [tra], [self.t_q16])
        self.tt("dve", rb3, x0, sinb, ALU.mult, rd, [trb])
        self.tt("dve", rb3b, x1, cosb, ALU.mult, rd, [trb])
        self.tt("dve", o[:, :, :, 1], rb3, rb3b, ALU.add, [trb], [self.t_q16])

    def memkv(self, s):
        dry = self.ws_dry
        if not dry:
            tiles = []
            for i in range(2):
                tmem = [self.t_fsc[4], self.t_fsc[5]]
                self.fw.dma("sp", self.memt, self.mem[s * MEM + i * 128: s * MEM + (i + 1) * 128, :], self.s_memt,
                            writes=tmem)
                ssn = self.stat[:, 32 + i:33 + i]
                self.memset("pool", ssn, 0.0, [self.t_ss])
                self.act(self.junk[:], self.memt, AF.Square, tmem, [self.t_junk, self.t_ss], accum=ssn)
                rsn = self.stat[:, 40 + i:41 + i]
                self.act(rsn, ssn, AF.Sqrt, [self.t_ss], [self.t_rs], scale=1.0 / D, bias=EPS)
                self.recip(rsn, rsn, [self.t_rs], [self.t_rs])
                self.ts("dve", self.xnb[:, i, :], self.memt, rsn, ALU.mult, tmem + [self.t_rs], [self.t_xnb[i]])
            for kcp in range(4):
                pt, tpt = self.bankT()
                for i in range(2):
                    for kk in range(2):
                        kc = kcp * 2 + kk
                        self.tr(pt[:, kk * GT + i * 128: kk * GT + (i + 1) * 128], self.xnb[:, i, kc * 128:(kc + 1) * 128],
                                [self.t_xnb[i]], tpt)
                src = pt[:].rearrange("p (k t) -> p k t", k=2)[:, :, 0:256]
                self.cp("dve" if kcp % 2 == 0 else "act", self.nT[:, 2 * kcp:2 * kcp + 2, 0:256], src, [tpt], [self.t_nT])
        for b in range(4):
            w, wt = self.wget(("wmem", b))
            if dry:
                continue
            wv = w[:, 0:2048].rearrange("p (k c) -> p k c", k=KC)
            if b < 2:
                for hh in range(2):
                    hc = b * 2 + hh
                    ps, tps = self.bank()
                    for kc in range(KC):
                        self.mm(ps[:, 0:256], wv[:, kc, hh * 128:(hh + 1) * 128], self.nT[:, kc, 0:256], kc == 0, kc == KC - 1,
                                [wt, self.t_nT], tps)
                    self.cp("act", self.KCT[:, hc, :], ps[:, 0:256], [tps], [self.t_KC])
            else:
                for mbk in range(2):
                    ps, tps = self.bank()
                    for kc in range(KC):
                        self.mm(ps[:, 0:256], self.nT[:, kc, mbk * 128:(mbk + 1) * 128], wv[:, kc, :], kc == 0, kc == KC - 1,
                                [wt, self.t_nT], tps)
                    self.cp("dve", self.VC[:, mbk, (b - 2) * 256:(b - 1) * 256], ps[:, 0:256], [tps], [self.t_VC])

    def pass1(self, s, off, S, g):
        dry = self.ws_dry
        tiles = None
        if not dry:
            tiles = self.xtiles()
            self.load_cs(g)
            self.norm_nT(tiles)
        self.ffn(0, tiles)
        if not dry:
            for i in range(4):
                r0 = off + g * GT + i * 128
                self.fw.dma("sp", self.h1s[r0:r0 + 128, :], tiles[i][0], self.s_x[(self.gpos * 4 - 4 + i) % 6],
                            reads=[tiles[i][1]], writes=[self.t_h1[r0 // 128]])
            self.norm_nT(tiles)
        w0, wt0 = self.wget(("wkv", 0))
        w1, wt1 = self.wget(("wkv", 1))
        if dry:
            return
        gg = (off + g * GT) // GT
        wv0 = w0[:, 0:2048].rearrange("p (k c) -> p k c", k=KC)
        wv1 = w1[:, 0:2048].rearrange("p (k c) -> p k c", k=KC)
        for i in range(4):
            blk = g * 4 + i
            ps, tps = self.bank()
            for kc in range(KC):
                self.mm(ps[:, 0:256], self.nT[:, kc, i * 128:(i + 1) * 128], wv0[:, kc, :], kc == 0, kc == KC - 1,
                        [wt0, self.t_nT], tps)
            self.cp("act", self.VA[:, blk, :, 0:64], ps[:, 128:256].rearrange("p (g d) -> p g d", g=2), [tps], [self.t_VA[g]])
            self.headnorm_rope(ps, tps, 2, self.gk, i)
            pt, tpt = self.bankT()
            self.tr(pt[:, 0:128], self.q16[:, 0:128], [self.t_q16], tpt)
            self.cp("act", self.KAT[:, blk * 128:(blk + 1) * 128], pt[:, 0:128], [tpt], [self.t_KA[g]])
            ps2, tps2 = self.bank()
            for kc in range(KC):
                self.mm(ps2[:, 0:128], self.nT[:, kc, i * 128:(i + 1) * 128], wv1[:, kc, 128:256], kc == 0, kc == KC - 1,
                        [wt1, self.t_nT], tps2)
            self.cp("dve", self.vbst[:, i, :, 0:64], ps2[:, 0:128].rearrange("p (g d) -> p g d", g=2), [tps2], [self.t_vbst])
        ps3, tps3 = self.bank()
        for kc in range(KC):
            self.mm(ps3[:], wv1[:, kc, 0:128], self.nT[:, kc, :], kc == 0, kc == KC - 1, [wt1, self.t_nT], tps3)
        self.cp("act", self.kbst[:], ps3[:], [tps3], [self.t_kbst])
        t0 = off + g * GT
        self.fw.dma("sp", self.kbs[:, t0:t0 + GT], self.kbst[:], self.s_kbst, reads=[self.t_kbst], writes=[self.t_kbs[gg]])
        self.fw.dma("sp", self.vbs[t0 // 128: t0 // 128 + 4].rearrange("b p c -> p b c"),
                    self.vbst[:].rearrange("p b g d -> p b (g d)"), self.s_vbst, reads=[self.t_vbst], writes=[self.t_vbs[gg]])

    def pass2(self, s, off, S, g):
        dry = self.ws_dry
        nblk = S // 128
        ngrp = S // GT
        tiles = None
        big = self.big
        tb = self.t_big
        QA, QB, QC, YA, YB, YC = 0, 4, 8, 12, 16, 20
        if not dry:
            tiles = self.xtiles()
            self.load_cs(g)
            wi = g % 2
            lo = max(0, 4 * g - 1)
            hi = min(nblk, 4 * g + 5)
            c0 = (lo - (4 * g - 1)) * 128
            gg0 = off // GT
            rds_k = [self.t_kbs[gg0 + x] for x in range(max(0, g - 1), min(ngrp, g + 2))]
            rds_v = [self.t_vbs[gg0 + x] for x in range(max(0, g - 1), min(ngrp, g + 2))]
            self.fw.dma("sp", self.KBw[wi][:, c0:c0 + (hi - lo) * 128], self.kbs[:, off + lo * 128: off + hi * 128],
                        self.s_KBw[wi], reads=rds_k, writes=[self.t_KBw[wi]])
            self.fw.dma("sp", self.VBw[wi][:, c0 // 128: c0 // 128 + (hi - lo), :],
                        self.vbs[off // 128 + lo: off // 128 + hi].rearrange("b p c -> p b c"),
                        self.s_VBw[wi], reads=rds_v, writes=[self.t_VBw[wi]])
            self.norm_nT(tiles)
        wqa = [self.wget(("wqa", b)) for b in range(2)]
        if not dry:
            va = [w[0][:, 0:2048].rearrange("p (k c) -> p k c", k=KC) for w in wqa]
            for i in range(4):
                blk = g * 4 + i
                ps, tps = self.bank()
                for b in range(2):
                    for kc in range(KC):
                        self.mm(ps[:, b * 256:(b + 1) * 256], self.nT[:, kc, i * 128:(i + 1) * 128], va[b][:, kc, :],
                                kc == 0, kc == KC - 1, [wqa[b][1], self.t_nT], tps)
                self.headnorm_rope(ps, tps, 8, self.gq, i)
                pt, tpt = self.bankT()
                for p in range(4):
                    self.tr(pt[:, p * 128:(p + 1) * 128], self.q16[:, p * 128:(p + 1) * 128], [self.t_q16], tpt)
                self.cp("act", big[:, QA:QA + 4, i * 128:(i + 1) * 128], pt[:, 0:512].rearrange("p (a t) -> p a t", a=4),
                        [tpt], tb[QA:QA + 4])
        for nm, qbase, ceng in (("wqb", QB, "dve"), ("wqc", QC, "act")):
            for b in range(2):
                w, wt = self.wget((nm, b))
                if dry:
                    continue
                wv = w[:, 0:2048].rearrange("p (k c) -> p k c", k=KC)
                for pp in range(2):
                    p = 2 * b + pp
                    ps, tps = self.bank()
                    for kc in range(KC):
                        self.mm(ps[:], wv[:, kc, pp * 128:(pp + 1) * 128], self.nT[:, kc, :], kc == 0, kc == KC - 1,
                                [wt, self.t_nT], tps)
                    self.cp(ceng, big[:, qbase + p, :], ps[:], [tps], [tb[qbase + p]])
        if not dry:
            kgr = lambda kb: kb // 4
            for p in range(4):
                acc = [self.bank(), self.bank()]
                sb = [self.bank() for _ in range(4)]
                pts = [None] * 4

                def qk(kb):
                    for gq in range(2):
                        sbk, tsbk = sb[(2 * kb + gq) % 4]
                        self.mm(sbk[:], self.KAT[64 * gq:64 * gq + 64, kb * 128:(kb + 1) * 128],
                                big[64 * gq:64 * gq + 64, QA + p, :], True, True, [self.t_KA[kgr(kb)], tb[QA + p]], tsbk)
                qk(0)
                for kb in range(nblk):
                    if kb + 1 < nblk:
                        qk(kb + 1)
                    for gq in range(2):
                        sbk, tsbk = sb[(2 * kb + gq) % 4]
                        ptile, tpt_ = self.nextPT()
                        pts[(2 * kb + gq) % 4] = (ptile, tpt_)
                        self.act(ptile[:], sbk[:], AF.Exp, [tsbk], [tpt_], scale=0.125)
                    for gq in range(2):
                        ptile, tpt_ = pts[(2 * kb + gq) % 4]
                        self.mm(acc[gq][0][:], self.VA[:, kb, gq, :], ptile[:], kb == 0, kb == nblk - 1,
                                [self.t_VA[kgr(kb)], tpt_], acc[gq][1])
                for gq in range(2):
                    rec, trec = self.nextF()
                    self.recip(rec[0:64, :], acc[gq][0][64:128, :], [acc[gq][1]], [trec])
                    self.tt("dve", big[64 * gq:64 * gq + 64, YA + p, :], acc[gq][0][0:64, :], rec[0:64, :], ALU.mult,
                            [acc[gq][1], trec], [tb[YA + p]])
            wi = g % 2
            KBw, VBw = self.KBw[wi], self.VBw[wi]
            for i in range(4):
                qb_ = 4 * g + i
                olist = [o for o in range(3) if 0 <= qb_ + o - 1 < nblk]
                for gq in range(2):
                    acc, tacc = self.bank()
                    for idx, o in enumerate(olist):
                        wblk = i + o
                        sbk, tsbk = self.bank()
                        self.mm(sbk[:], KBw[64 * gq:64 * gq + 64, wblk * 128:(wblk + 1) * 128],
                                big[64 * gq:64 * gq + 64, QB:QB + 4, i * 128:(i + 1) * 128], True, True,
                                [self.t_KBw[wi]] + tb[QB:QB + 4], tsbk)
                        sf, tsf = self.nextF()
                        self.stt(sf[:], sbk[:], 0.125, self.bias[:, o, 4 * gq:4 * gq + 4, :].rearrange("p h i -> p (h i)"),
                                 ALU.mult, ALU.add, [tsbk, self.t_bias], [tsf])
                        ptile, tpt_ = self.nextPT()
                        self.act(ptile[:], sf[:], AF.Exp, [tsf], [tpt_])
                        self.mm(acc[:], VBw[:, wblk, gq * 128:(gq + 1) * 128], ptile[:], idx == 0, idx == len(olist) - 1,
                                [self.t_VBw[wi], tpt_], tacc)
                    den, tden = self.nextF()
                    self.tt("dve", den[64:128, :].rearrange("p (h i) -> p h i", h=4),
                            acc[64:128, :].rearrange("p (h i) -> p h i", h=4),
                            self.esk[64:128, 4 * gq:4 * gq + 4].unsqueeze(2).to_broadcast([64, 4, 128]), ALU.add,
                            [tacc, self.t_esk], [tden])
                    self.recip(den[0:64, :], den[64:128, :], [tden], [tden])
                    self.tt("dve", big[64 * gq:64 * gq + 64, YB:YB + 4, i * 128:(i + 1) * 128],
                            acc[0:64, :].rearrange("p (h i) -> p h i", h=4), den[0:64, :].rearrange("p (h i) -> p h i", h=4),
                            ALU.mult, [tacc, tden], tb[YB:YB + 4])
            for hc in range(4):
                accO, taO = self.bank()
                accS, taS = self.bank()
                for mbk in range(2):
                    sbk, tsbk = self.bank()
                    self.mm(sbk[:], self.KCT[:, hc, mbk * 128:(mbk + 1) * 128], big[:, QC + hc, :], True, True,
                            [self.t_KC, tb[QC + hc]], tsbk)
                    ptile, tpt_ = self.nextPT()
                    self.act(ptile[:], sbk[:], AF.Exp, [tsbk], [tpt_], scale=128 ** -0.5)
                    self.mm(accO[:], self.VC[:, mbk, hc * 128:(hc + 1) * 128], ptile[:], mbk == 0, mbk == 1,
                            [self.t_VC, tpt_], taO)
                    self.mm(accS[:], self.ones[:], ptile[:], mbk == 0, mbk == 1, [self.t_ones, tpt_], taS)
                rec, trec = self.nextF()
                self.recip(rec[:], accS[:], [taS], [trec])
                self.tt("dve", big[:, YC + hc, :], accO[:], rec[:], ALU.mult, [taO, trec], [tb[YC + hc]])
        if self.debug and not dry and g == 0 and s == 0:
            self.out_ops.append(self.fw.dma("sp", self.dbg_big, self.big[:].rearrange("p a t -> p (a t)"), self.fw.new_dma_sem(),
                                            reads=self.t_big))
        for f in range(8):
            macc = tmacc = None
            for bi in range(3):
                w, wt = self.wget(("wm", f, bi))
                if dry:
                    continue
                gv = w[:, 0:1024].rearrange("p (k c) -> p k c", k=KC)
                bv = w[:, 1024:1536].rearrange("p (k c) -> p k c", k=4)
                ybase = (YA, YB, YC)[bi]
                pz, tpz = self.bank()
                pg, tpg = self.bank()
                for p in range(4):
                    self.mm(pz[:], bv[:, p, :], big[:, ybase + p, :], p == 0, p == 3, [wt, tb[ybase + p]], tpz)
                for kc in range(KC):
                    self.mm(pg[:], gv[:, kc, :], self.nT[:, kc, :], kc == 0, kc == KC - 1, [wt, self.t_nT], tpg)
                sg, tsg = self.nextF()
                self.act(sg[:], pg[:], AF.Tanh, [tpg], [tsg], scale=0.5)
                if bi == 0:
                    macc, tmacc = self.nextF()
                    self.stt(macc[:], sg[:], 1.0, pz[:], ALU.add, ALU.mult, [tsg, tpz], [tmacc])
                else:
                    mt, tmt = self.nextF()
                    self.stt(mt[:], sg[:], 1.0, pz[:], ALU.add, ALU.mult, [tsg, tpz], [tmt])
                    if bi == 1:
                        self.tt("dve", macc[:], macc[:], mt[:], ALU.add, [tmacc, tmt], [tmacc])
                    else:
                        self.tt("dve", big[:, f, :], macc[:], mt[:], ALU.add, [tmacc, tmt], [tb[f]])
        for c in range(2):
            accs = None if dry else [self.bank() for _ in range(4)]
            for kh in range(2):
                w, wt = self.wget(("wo", c, kh))
                if dry:
                    continue
                wv = w[:, 0:2048].rearrange("p (k c) -> p k c", k=4)
                for kk in range(4):
                    kc = kh * 4 + kk
                    for i in range(4):
                        self.mm(accs[i][0][:], big[:, kc, i * 128:(i + 1) * 128], wv[:, kk, :], kc == 0, kc == KC - 1,
                                [wt, tb[kc]], accs[i][1])
            if dry:
                continue
            for i in range(4):
                ap, trk = tiles[i]
                self.stt(ap[:, c * 512:(c + 1) * 512], accs[i][0][:], 0.5, ap[:, c * 512:(c + 1) * 512], ALU.mult, ALU.add,
                         [accs[i][1], trk], [trk])
        if self.debug and not dry and g == 0 and s == 0:
            for i in range(4):
                self.out_ops.append(self.fw.dma("sp", self.dbg_h2[i * 128:(i + 1) * 128, :], tiles[i][0], self.fw.new_dma_sem(),
                                                reads=[tiles[i][1]]))
        if not dry:
            self.norm_nT(tiles)
        self.ffn(1, tiles)
        if dry:
            return
        ss = self.stat[:, 48:52]
        rs = self.stat[:, 56:60]
        self.memset("pool", ss, 0.0, [self.t_ss])
        for i, (ap, trk) in enumerate(tiles):
            self.act(self.junk[:], ap, AF.Square, [trk], [self.t_junk, self.t_ss], accum=self.stat[:, 48 + i:49 + i])
        self.act(rs, ss, AF.Sqrt, [self.t_ss], [self.t_rs], scale=1.0 / D, bias=EPS)
        self.recip(rs, rs, [self.t_rs], [self.t_rs])
        for i, (ap, trk) in enumerate(tiles):
            self.stt(ap, ap, self.stat[:, 56 + i:57 + i], self.gfin[:], ALU.mult, ALU.mult, [trk, self.t_rs, self.t_gfin], [trk])
            r0 = off + g * GT + i * 128
            op = self.fw.dma("sp", self.y[r0:r0 + 128, :], ap, self.s_x[(self.gpos * 4 - 4 + i) % 6], reads=[trk])
            self.out_ops.append(op)


_PROG_CACHE = {}


def _get_program(seqs, debug=False):
    key = tuple(seqs) + (debug,)
    if key not in _PROG_CACHE:
        b = Builder(seqs, debug)
        nc = b.build()
        _PROG_CACHE[key] = (nc, b)
    return _PROG_CACHE[key]


def run_cores(per_core_x, per_core_mem, weights, seqs, debug=False):
    nc, b = _get_program(seqs, debug)
    cs, oh, ident = _host_constants(b.maxblk)
    in_maps = []
    for xc, mc in zip(per_core_x, per_core_mem):
        m = {"x": np.ascontiguousarray(xc, dtype=np.float32), "mem": np.ascontiguousarray(mc, dtype=np.float32),
             "c_cs": cs, "c_oh": oh, "c_ident": ident}
        for k, v in weights.items():
            m[k] = np.ascontiguousarray(v, dtype=np.float32)
        in_maps.append(m)
    res = run_bass_kernel_spmd(nc, in_maps, core_ids=list(range(len(in_maps))))
    if debug:
        return res.results
    return [r["y"] for r in res.results]


def kernel(x_prompt, x_sample, mem_prompt, mem_sample, rel_bias, norm_ffn1, ffn1_w_in, ffn1_w_out, norm_mix, w_in,
           q_norm_a, k_norm_a, sink_b, norm_mem, w_mem_kv, w_br_a, w_br_b, w_br_c, w_out, norm_ffn2, ffn2_w_in,
           ffn2_w_out, norm_final):
    x_prompt = np.asarray(x_prompt)
    x_sample = np.asarray(x_sample)
    mem_prompt = np.asarray(mem_prompt)
    mem_sample = np.asarray(mem_sample)
    ncores = 8
    BP, SP, _ = x_prompt.shape
    BS, SS, _ = x_sample.shape
    per = BS // ncores
    seqs = [SP] + [SS] * per
    weights = dict(rel_bias=rel_bias, norm_ffn1=norm_ffn1, ffn1_w_in=ffn1_w_in, ffn1_w_out=ffn1_w_out, norm_mix=norm_mix,
                   w_in=w_in, q_norm_a=q_norm_a, k_norm_a=k_norm_a, sink_b=sink_b, norm_mem=norm_mem, w_mem_kv=w_mem_kv,
                   w_br_a=w_br_a, w_br_b=w_br_b, w_br_c=w_br_c, w_out=w_out, norm_ffn2=norm_ffn2, ffn2_w_in=ffn2_w_in,
                   ffn2_w_out=ffn2_w_out, norm_final=norm_final)
    weights = {k: np.asarray(v) for k, v in weights.items()}
    xs, ms = [], []
    for c in range(ncores):
        xs.append(np.concatenate([x_prompt[c]] + [x_sample[c * per + i] for i in range(per)], axis=0))
        ms.append(np.concatenate([mem_prompt[c]] + [mem_sample[c * per + i] for i in range(per)], axis=0))
    ys = run_cores(xs, ms, weights, seqs)
    y_prompt = np.stack([ys[c][0:SP] for c in range(ncores)], axis=0).astype(np.float32)
    y_sample = np.stack([ys[c][SP + i * SS: SP + (i + 1) * SS] for c in range(ncores) for i in range(per)], axis=0)
    return (y_prompt, y_sample.astype(np.float32))
```

```python
import math
import contextlib
import numpy as np
import concourse.bass as bass
import concourse.mybir as mybir
from concourse.bass_utils import run_bass_kernel_spmd

F32 = mybir.dt.float32
BF16 = mybir.dt.bfloat16
AF = mybir.ActivationFunctionType
ALU = mybir.AluOpType
AX = mybir.AxisListType

D = 1024
DFF = 2816
NJ = 22
KC = 8
GT = 512
EPS = 1e-6
NEG = -30000.0
MEM = 256
COMPUTE = ("pe", "act", "dve", "pool")


class Trk:
    __slots__ = ("name", "w", "r")

    def __init__(self, name=""):
        self.name = name
        self.w = None
        self.r = []


class Op:
    __slots__ = ("eng", "fn", "waits", "inc", "idx", "dma_sem", "dma_val", "is_dma")

    def __init__(self, eng, fn, is_dma=False):
        self.eng = eng
        self.fn = fn
        self.waits = []
        self.inc = False
        self.idx = -1
        self.is_dma = is_dma
        self.dma_sem = None
        self.dma_val = 0


class FW:
    def __init__(self, nc, stack):
        self.nc = nc
        self.stack = stack
        self.ops = {e: [] for e in ("pe", "act", "dve", "pool", "sp")}
        self.sems = {}
        for e in COMPUTE:
            self.sems[e] = stack.enter_context(nc.semaphore("sem_" + e))
        self.waited = {e: {s: -1 for s in COMPUTE} for e in self.ops}
        self.waited_dma = {e: {} for e in self.ops}
        self.n_dma_sem = 0

    def sbuf(self, name, shape, dt):
        return self.stack.enter_context(self.nc.sbuf_tensor(name, list(shape), dt))

    def psum(self, name, shape, dt):
        return self.stack.enter_context(self.nc.psum_tensor(name, list(shape), dt))

    def new_dma_sem(self):
        self.n_dma_sem += 1
        h = self.stack.enter_context(self.nc.semaphore(f"dsem{self.n_dma_sem}"))
        return [h, 0]

    def _deps(self, op, reads, writes):
        e = op.eng
        best = {}
        bestd = {}
        def consider(d, kind):
            if d is None or d is op:
                return
            if d.is_dma:
                sid = id(d.dma_sem)
                if sid not in bestd or d.dma_val > bestd[sid].dma_val:
                    bestd[sid] = d
                return
            if d.eng == e and not op.is_dma:
                if e == "pe" or kind != "raw":
                    return
            if d.eng not in best or d.idx > best[d.eng].idx:
                best[d.eng] = d
        for t in reads:
            consider(t.w, "raw")
        for t in writes:
            consider(t.w, "waw")
            for r in t.r:
                consider(r, "war")
        for sid, d in bestd.items():
            if d.dma_val > self.waited_dma[e].get(sid, 0):
                self.waited_dma[e][sid] = d.dma_val
                op.waits.append(d)
        for se, d in best.items():
            if d.idx > self.waited[e][se]:
                self.waited[e][se] = d.idx
                d.inc = True
                op.waits.append(d)
        for t in writes:
            t.w = op
            t.r = []
        for t in reads:
            if t.w is not op:
                t.r.append(op)

    def emit(self, eng, fn, reads=(), writes=()):
        op = Op(eng, fn)
        op.idx = len(self.ops[eng])
        self._deps(op, reads, writes)
        self.ops[eng].append(op)
        return op

    def dma(self, eng, out, in_, dsem, reads=(), writes=(), skip_deps=False, **kw):
        def fn(h):
            return h.dma_start(out=out, in_=in_, **kw)
        op = Op(eng, fn, is_dma=True)
        op.idx = len(self.ops[eng])
        if skip_deps:
            for t in writes:
                t.w = op
                t.r = []
        else:
            self._deps(op, reads, writes)
        dsem[1] += 16
        op.dma_sem = dsem
        op.dma_val = dsem[1]
        self.ops[eng].append(op)
        return op

    def finish(self, final_waits=()):
        nc = self.nc
        val = {}
        for e in COMPUTE:
            c = 0
            for op in self.ops[e]:
                if op.is_dma:
                    continue
                if op.inc:
                    c += 1
                    val[id(op)] = c
        fw = self

        def do_wait(h, d):
            if d.is_dma:
                h.wait_ge(d.dma_sem[0], d.dma_val)
            else:
                h.wait_ge(fw.sems[d.eng], val[id(d)])

        def run(e, h):
            for op in fw.ops[e]:
                for d in op.waits:
                    do_wait(h, d)
                ins = op.fn(h)
                if op.is_dma:
                    ins.then_inc(op.dma_sem[0], 16)
                elif op.inc:
                    ins.then_inc(fw.sems[e], 1)
            if e == "sp":
                for d in final_waits:
                    do_wait(h, d)

        with nc.Block() as block:
            @block.tensor
            def _(h):
                run("pe", h)

            @block.scalar
            def _(h):
                run("act", h)

            @block.vector
            def _(h):
                run("dve", h)

            @block.gpsimd
            def _(h):
                run("pool", h)

            @block.sync
            def _(h):
                run("sp", h)
        return {e: len(self.ops[e]) for e in self.ops}


def _t5_bucket_np(rel):
    rel = np.asarray(rel, dtype=np.int64)
    nb = 16
    max_exact = 8
    ret = np.where(rel > 0, nb, 0)
    n = np.abs(rel)
    nf = np.maximum(n, 1).astype(np.float32)
    large = max_exact + (np.log(nf / np.float32(max_exact)) / np.float32(math.log(128 / max_exact))
                         * np.float32(nb - max_exact)).astype(np.int32)
    large = np.minimum(large, nb - 1)
    return ret + np.where(n < max_exact, n, large)


def _host_constants(max_blocks):
    t = (np.arange(max_blocks)[None, :] * 128 + np.arange(128)[:, None]).astype(np.float32)
    row = np.floor(t / 64.0).astype(np.float32)
    col = (t - row * 64.0).astype(np.float32)
    inv = (np.float32(10000.0) ** (-np.arange(0, 32, 2, dtype=np.float32) / np.float32(32))).astype(np.float32)
    ang = np.concatenate([row[:, :, None] * inv[None, None, :], col[:, :, None] * inv[None, None, :]], axis=-1)
    ang = ang.astype(np.float32)
    cs = np.concatenate([np.cos(ang), np.sin(ang)], axis=-1).astype(np.float32)
    j = np.arange(128)[:, None, None]
    o = np.arange(3)[None, :, None]
    i = np.arange(128)[None, None, :]
    rel = o * 128 + j - 128 - i
    bucket = _t5_bucket_np(rel)
    valid = np.abs(rel) <= 128
    oh = np.zeros((33, 128, 3, 128), dtype=np.float32)
    for b in range(32):
        oh[b] = ((bucket == b) & valid).astype(np.float32)
    oh[32] = (~valid).astype(np.float32)
    ident = np.eye(128, dtype=np.float32)
    return cs, oh.reshape(33, 128, 384), ident


class Builder:
    def __init__(self, seqs, debug=False):
        self.debug = debug
        self.seqs = list(seqs)
        self.ntok = sum(seqs)
        self.nseq = len(seqs)
        self.maxblk = max(seqs) // 128
        self.pb = 0

    def mm(self, out, lhsT, rhs, start, stop, reads, wtrk):
        self.fw.emit("pe", lambda h: h.matmul(out, lhsT=lhsT, rhs=rhs, start=start, stop=stop), reads, [wtrk])

    def tr(self, out, in_, reads, wtrk):
        ident = self.ident[:]
        self.fw.emit("pe", lambda h: h.transpose(out=out, in_=in_, identity=ident), list(reads) + [self.t_ident], [wtrk])

    def act(self, out, in_, func, reads, writes, scale=1.0, bias=0.0, accum=None):
        if accum is None:
            self.fw.emit("act", lambda h: h.activation(out=out, in_=in_, func=func, bias=bias, scale=scale), reads, writes)
        else:
            self.fw.emit("act", lambda h: h.activation(out=out, in_=in_, func=func, bias=bias, scale=scale,
                                                       accum_out=accum), reads, writes)

    def tt(self, eng, out, in0, in1, op, reads, writes):
        self.fw.emit(eng, lambda h: h.tensor_tensor(out=out, in0=in0, in1=in1, op=op), reads, writes)

    def stt(self, out, in0, scalar, in1, op0, op1, reads, writes):
        self.fw.emit("dve", lambda h: h.scalar_tensor_tensor(out=out, in0=in0, scalar=scalar, in1=in1, op0=op0, op1=op1),
                     reads, writes)

    def ts(self, eng, out, in0, s1, op0, reads, writes, s2=None, op1=None):
        if op1 is None:
            self.fw.emit(eng, lambda h: h.tensor_scalar(out=out, in0=in0, scalar1=s1, scalar2=None, op0=op0), reads, writes)
        else:
            self.fw.emit(eng, lambda h: h.tensor_scalar(out=out, in0=in0, scalar1=s1, scalar2=s2, op0=op0, op1=op1),
                         reads, writes)

    def cp(self, eng, out, in_, reads, writes):
        if eng == "act":
            self.fw.emit("act", lambda h: h.copy(out=out, in_=in_), reads, writes)
        else:
            self.fw.emit(eng, lambda h: h.tensor_copy(out=out, in_=in_), reads, writes)

    def recip(self, out, in_, reads, writes):
        self.fw.emit("dve", lambda h: h.reciprocal(out=out, in_=in_), reads, writes)

    def memset(self, eng, ap, v, writes):
        self.fw.emit(eng, lambda h: h.memset(ap, v), [], writes)

    def bank(self):
        b = self.pb % 6
        self.pb += 1
        return self.psF[b], self.t_psF[b]

    def bankT(self):
        b = self.pbT % 2
        self.pbT += 1
        return self.psT[b], self.t_psT[b]

    def build(self):
        nc = bass.Bass("TRN2", target_bir_lowering=False)
        self.nc = nc
        NT = self.ntok
        dt_in = lambda n, s: nc.dram_tensor(n, list(s), F32, kind="ExternalInput").ap()
        self.x = dt_in("x", [NT, D])
        self.mem = dt_in("mem", [self.nseq * MEM, D])
        self.rel_bias = dt_in("rel_bias", [32, 8])
        self.norm_ffn1 = dt_in("norm_ffn1", [1, D])
        self.ffn1_w_in = dt_in("ffn1_w_in", [1, D, 2 * DFF])
        self.ffn1_w_out = dt_in("ffn1_w_out", [1, DFF, D])
        self.norm_mix = dt_in("norm_mix", [1, D])
        self.w_in = dt_in("w_in", [1, D, 5120])
        self.q_norm_a = dt_in("q_norm_a", [1, 64])
        self.k_norm_a = dt_in("k_norm_a", [1, 64])
        self.sink_b = dt_in("sink_b", [1, 8])
        self.norm_mem = dt_in("norm_mem", [1, D])
        self.w_mem_kv = dt_in("w_mem_kv", [1, D, D])
        self.w_br = [dt_in("w_br_a", [1, 512, D]), dt_in("w_br_b", [1, 512, D]), dt_in("w_br_c", [1, 512, D])]
        self.w_out = dt_in("w_out", [1, D, D])
        self.norm_ffn2 = dt_in("norm_ffn2", [1, D])
        self.ffn2_w_in = dt_in("ffn2_w_in", [1, D, 2 * DFF])
        self.ffn2_w_out = dt_in("ffn2_w_out", [1, DFF, D])
        self.norm_final = dt_in("norm_final", [D])
        self.c_cs = dt_in("c_cs", [128, self.maxblk, 64])
        self.c_oh = dt_in("c_oh", [33, 128, 384])
        self.c_ident = dt_in("c_ident", [128, 128])
        self.y = nc.dram_tensor("y", [NT, D], F32, kind="ExternalOutput").ap()
        if self.debug:
            self.h1s = nc.dram_tensor("h1s", [NT, D], F32, kind="ExternalOutput").ap()
            self.dbg_big = nc.dram_tensor("dbg_big", [128, 24 * GT], BF16, kind="ExternalOutput").ap()
            self.dbg_h2 = nc.dram_tensor("dbg_h2", [GT, D], F32, kind="ExternalOutput").ap()
        else:
            self.h1s = nc.dram_tensor("h1s", [NT, D], F32).ap()
        self.kbs = nc.dram_tensor("kbs", [128, NT], BF16).ap()
        self.vbs = nc.dram_tensor("vbs", [NT // 128, 128, 256], BF16).ap()
        self.wsc = {}

        with contextlib.ExitStack() as st:
            self.fw = fw = FW(nc, st)
            self.alloc()
            self.setup()
            self.ws_plan = []
            self.ws_dry = True
            self.main()
            self.ws_dry = False
            self.ws_issued = 0
            self.ws_consumed = 0
            self.pb = 0
            self.pbT = 0
            self.main()
            counts = fw.finish(final_waits=self.out_ops)
        self.counts = counts
        return nc

    def alloc(self):
        fw = self.fw
        self.xring = fw.sbuf("xring", [128, 6, D], F32)
        self.t_x = [Trk(f"x{i}") for i in range(6)]
        self.s_x = [fw.new_dma_sem() for _ in range(6)]
        self.wring = [fw.sbuf(f"wring{i}", [128, 2048], BF16) for i in range(6)]
        self.t_w = [Trk(f"w{i}") for i in range(6)]
        self.s_w = [fw.new_dma_sem() for _ in range(6)]
        self.nT = fw.sbuf("nT", [128, KC, GT], BF16)
        self.t_nT = Trk("nT")
        self.big = fw.sbuf("big", [128, 24, GT], BF16)
        self.t_big = [Trk(f"big{i}") for i in range(24)]
        self.xnb = fw.sbuf("xnb", [128, 4, D], BF16)
        self.t_xnb = [Trk(f"xnb{i}") for i in range(4)]
        self.junk = fw.sbuf("junk", [128, D], BF16)
        self.t_junk = Trk("junk")
        mb = self.maxblk
        self.KAT = fw.sbuf("KAT", [128, mb * 128], BF16)
        self.VA = fw.sbuf("VA", [128, mb, 2, 128], BF16)
        self.t_KA = [Trk(f"KA{i}") for i in range(mb // 4)]
        self.t_VA = [Trk(f"VA{i}") for i in range(mb // 4)]
        self.KBw = [fw.sbuf(f"KBw{i}", [128, 768], BF16) for i in range(2)]
        self.VBw = [fw.sbuf(f"VBw{i}", [128, 6, 256], BF16) for i in range(2)]
        self.t_KBw = [Trk() for _ in range(2)]
        self.t_VBw = [Trk() for _ in range(2)]
        self.s_KBw = [fw.new_dma_sem() for _ in range(2)]
        self.s_VBw = [fw.new_dma_sem() for _ in range(2)]
        self.kbst = fw.sbuf("kbst", [128, GT], BF16)
        self.vbst = fw.sbuf("vbst", [128, 4, 2, 128], BF16)
        self.t_kbst = Trk()
        self.t_vbst = Trk()
        self.s_kbst = fw.new_dma_sem()
        self.s_vbst = fw.new_dma_sem()
        self.KCT = fw.sbuf("KCT", [128, 4, MEM], BF16)
        self.VC = fw.sbuf("VC", [128, 2, 512], BF16)
        self.t_KC = Trk()
        self.t_VC = Trk()
        self.s_memt = fw.new_dma_sem()
        self.PT = [fw.sbuf(f"PT{i}", [128, GT], BF16) for i in range(6)]
        self.t_PT = [Trk() for _ in range(6)]
        self.ipt = 0
        self.fsc_all = fw.sbuf("fsc_all", [128, 8, GT], F32)
        self.fsc = [self.fsc_all[:, i, :] for i in range(8)]
        self.t_fsc = [Trk() for _ in range(8)]
        self.memt = self.fsc_all[:, 4:6, :].rearrange("p a t -> p (a t)")
        self.ifs = 0
        self.q16 = fw.sbuf("q16", [128, GT], BF16)
        self.t_q16 = Trk()
        self.stat = fw.sbuf("stat", [128, 64], F32)
        self.t_ss = Trk()
        self.t_rs = Trk()
        self.t_ssq = Trk()
        self.t_rsq = Trk()
        self.bias = fw.sbuf("bias", [128, 3, 8, 128], F32)
        self.t_bias = Trk()
        self.csg = [fw.sbuf(f"csg{i}", [128, 4, 64], F32) for i in range(2)]
        self.t_csg = [Trk() for _ in range(2)]
        self.s_csg = [fw.new_dma_sem() for _ in range(2)]
        self.ics = 0
        self.ident = fw.sbuf("ident", [128, 128], BF16)
        self.t_ident = Trk()
        self.ones = fw.sbuf("ones", [128, 128], BF16)
        self.t_ones = Trk()
        self.gq = fw.sbuf("gq", [128, 64], F32)
        self.gk = fw.sbuf("gk", [128, 64], F32)
        self.t_gqk = Trk()
        self.gfin = fw.sbuf("gfin", [128, D], F32)
        self.t_gfin = Trk()
        self.esk = fw.sbuf("esk", [128, 8], F32)
        self.t_esk = Trk()
        self.gains = fw.sbuf("gains", [128, 4, KC], F32)
        self.t_gains = Trk()
        self.rbb = fw.sbuf("rbb", [128, 256], F32)
        self.t_rbb = Trk()
        self.psF = [fw.psum(f"psF{i}", [128, GT], F32) for i in range(6)]
        self.t_psF = [Trk(f"psF{i}") for i in range(6)]
        self.psT = [fw.psum(f"psT{i}", [128, 2 * GT], BF16) for i in range(2)]
        self.t_psT = [Trk(f"psT{i}") for i in range(2)]
        self.pbT = 0
        self.t_h1 = [Trk() for _ in range(self.ntok // 128)]
        self.t_kbs = [Trk() for _ in range(self.ntok // GT)]
        self.t_vbs = [Trk() for _ in range(self.ntok // GT)]
        self.out_ops = []

    def nextPT(self):
        i = self.ipt % 6
        self.ipt += 1
        return self.PT[i], self.t_PT[i]

    def nextF(self):
        i = self.ifs % 8
        self.ifs += 1
        return self.fsc[i], self.t_fsc[i]

    def setup(self):
        fw = self.fw
        nc = self.nc
        sem = fw.new_dma_sem

        def ld(eng, out, in_, trk, **kw):
            fw.dma(eng, out, in_, sem(), writes=[trk], **kw)
        ld("pool", self.ident[:], self.c_ident, self.t_ident)
        ld("sp", self.gq[:], self.q_norm_a.partition_broadcast(128).rearrange("p a b -> p (a b)"), self.t_gqk)
        ld("sp", self.gk[:], self.k_norm_a.partition_broadcast(128).rearrange("p a b -> p (a b)"), self.t_gqk)
        ld("sp", self.gfin[:], self.norm_final.partition_broadcast(128), self.t_gfin)
        ld("sp", self.esk[:], self.sink_b.partition_broadcast(128).rearrange("p a b -> p (a b)"), self.t_esk)
        ld("sp", self.rbb[:], self.rel_bias.rearrange("b h -> (b h)").partition_broadcast(128), self.t_rbb)
        for gi, g in enumerate([self.norm_ffn1, self.norm_mix, self.norm_ffn2, self.norm_mem]):
            ld("sp", self.gains[:, gi, :], g[0].rearrange("(kc p) -> p kc", p=128), self.t_gains,
               allow_slow_non_contiguous=True)
        self.act(self.esk[:], self.esk[:], AF.Exp, [self.t_esk], [self.t_esk])
        self.memset("pool", self.ones[:], 1.0, [self.t_ones])
        self.memset("pool", self.VA[:, :, :, 64:128], 1.0, self.t_VA)
        self.memset("pool", self.vbst[:, :, :, 64:128], 1.0, [self.t_vbst])
        self.memset("dve", self.bias[:], 0.0, [self.t_bias])
        for b in range(33):
            buf, tb = self.nextF()
            s = sem()
            fw.dma("sp", buf[:, 0:384], self.c_oh[b], s, writes=[tb])
            for h in range(8):
                sc = NEG if b == 32 else self.rbb[:, b * 8 + h: b * 8 + h + 1]
                self.stt(self.bias[:, :, h, :], buf[:, 0:384].rearrange("p (o i) -> p o i", o=3), sc,
                         self.bias[:, :, h, :], ALU.mult, ALU.add, [tb, self.t_bias, self.t_rbb], [self.t_bias])
        self.prep_list = []

        def wsrc(w):
            return w[0].rearrange("(kc p) c -> p kc c", p=128)
        G1, GM, G2, GME = 0, 1, 2, 3
        for which, (win, wout, gi) in enumerate([(self.ffn1_w_in, self.ffn1_w_out, G1), (self.ffn2_w_in, self.ffn2_w_out, G2)]):
            wr = wsrc(win)
            for j in range(NJ):
                self.prep_block(("w1", which, j), 2048,
                                [((0, 128), wr[:, :, j * 128:(j + 1) * 128]),
                                 ((128, 256), wr[:, :, DFF + j * 128: DFF + (j + 1) * 128])], 256, gi, 2048)
            wo = wout[0].rearrange("(j p) c -> p j c", p=128)
            for c in range(2):
                for jb in range(6):
                    nj = 4 if jb < 5 else 2
                    self.prep_block(("w2", which, c, jb), nj * 512,
                                    [((0, 512), wo[:, jb * 4: jb * 4 + nj, c * 512:(c + 1) * 512])], 512, None, 0)
        wi = wsrc(self.w_in)
        self.prep_block(("wkv", 0), 2048, [((0, 256), wi[:, :, 512:768])], 256, GM, 2048)
        self.prep_block(("wkv", 1), 2048, [((0, 256), wi[:, :, 1280:1536])], 256, GM, 2048)
        for blk in range(2):
            for nm, base in (("wqa", 0), ("wqb", 768)):
                pieces = []
                for pp in range(2):
                    p = blk * 2 + pp
                    pieces.append(((pp * 128, pp * 128 + 64), wi[:, :, base + p * 64: base + p * 64 + 64]))
                    pieces.append(((pp * 128 + 64, pp * 128 + 128), wi[:, :, base + (p + 4) * 64: base + (p + 4) * 64 + 64]))
                self.prep_block((nm, blk), 2048, pieces, 256, GM, 2048)
            self.prep_block(("wqc", blk), 2048, [((0, 256), wi[:, :, 1536 + blk * 256: 1536 + (blk + 1) * 256])], 256, GM, 2048)
        for f in range(8):
            for bi in range(3):
                key = ("wm", f, bi)
                gbase = 2048 + bi * 1024
                wb = self.w_br[bi][0]
                pieces = [((0, 128), wi[:, :, gbase + f * 128: gbase + (f + 1) * 128])]
                extra = []
                if bi < 2:
                    v = wb.rearrange("(h d) c -> d h c", d=64)
                    extra.append((0, 64, v[:, 0:4, f * 128:(f + 1) * 128]))
                    extra.append((64, 128, v[:, 4:8, f * 128:(f + 1) * 128]))
                else:
                    v = wb.rearrange("(h d) c -> d h c", d=128)
                    extra.append((0, 128, v[:, :, f * 128:(f + 1) * 128]))
                self.prep_block(key, 1536, pieces, 128, GM, 1024, extra=extra)
        wo = wsrc(self.w_out)
        for c in range(2):
            for kh in range(2):
                self.prep_block(("wo", c, kh), 2048, [((0, 512), wo[:, kh * 4:(kh + 1) * 4, c * 512:(c + 1) * 512])], 512, None, 0)
        wm = wsrc(self.w_mem_kv)
        for b in range(4):
            self.prep_block(("wmem", b), 2048, [((0, 256), wm[:, :, b * 256:(b + 1) * 256])], 256, GME, 2048)
        self.prep_emit()

    def prep_block(self, key, ncols, pieces, width, gain_idx, gain_cols, extra=()):
        self.prep_list.append((key, ncols, pieces, width, gain_idx, gain_cols, extra))

    def prep_emit(self):
        fw = self.fw
        L = self.prep_list
        n = len(L)

        def stage(i):
            sp = i % 3
            stg = self.xring[:, 2 * sp: 2 * sp + 2, :].rearrange("p a d -> p (a d)")
            return stg, [self.t_x[2 * sp], self.t_x[2 * sp + 1]], self.s_x[2 * sp]

        def emit_in(i):
            key, ncols, pieces, width, gain_idx, gain_cols, extra = L[i]
            stg, tst, ssem = stage(i)
            first = True
            for (c0, c1), src in pieces:
                K = src.shape[1]
                dst = stg[:, 0:K * width].rearrange("p (k c) -> p k c", k=K)[:, :, c0:c1]
                fw.dma("sp", dst, src, ssem, writes=tst, skip_deps=not first)
                first = False
            for (p0, p1, src) in extra:
                dst = stg[p0:p1, 1024:1536].rearrange("p (k c) -> p k c", k=4)
                fw.dma("sp", dst, src, ssem, writes=tst, skip_deps=True)

        def emit_conv_out(i):
            key, ncols, pieces, width, gain_idx, gain_cols, extra = L[i]
            stg, tst, ssem = stage(i)
            ws = i % 6
            wsl = self.wring[ws]
            eng = ("dve", "act", "pool")[i % 3]
            if gain_idx is not None:
                gw = gain_cols // KC
                if eng == "act":
                    for kc in range(KC):
                        self.act(wsl[:, kc * gw:(kc + 1) * gw], stg[:, kc * gw:(kc + 1) * gw], AF.Copy,
                                 tst + [self.t_gains], [self.t_w[ws]], scale=self.gains[:, gain_idx, kc:kc + 1])
                else:
                    gap = self.gains[:, gain_idx, :].unsqueeze(2).to_broadcast([128, KC, gw])
                    self.tt(eng, wsl[:, 0:gain_cols].rearrange("p (k c) -> p k c", k=KC),
                            stg[:, 0:gain_cols].rearrange("p (k c) -> p k c", k=KC), gap, ALU.mult,
                            tst + [self.t_gains], [self.t_w[ws]])
                if ncols > gain_cols:
                    self.cp("act", wsl[:, gain_cols:ncols], stg[:, gain_cols:ncols], tst, [self.t_w[ws]])
            else:
                self.cp(eng, wsl[:, 0:ncols], stg[:, 0:ncols], tst, [self.t_w[ws]])
            dr = self.nc.dram_tensor("wsc_" + "_".join(str(k) for k in key), [128, ncols], BF16).ap()
            trk = Trk()
            fw.dma("sp", dr, wsl[:, 0:ncols], self.s_w[ws], reads=[self.t_w[ws]], writes=[trk])
            self.wsc[key] = (dr, trk, ncols)

        for i in range(min(2, n)):
            emit_in(i)
        for i in range(n):
            if i + 2 < n:
                emit_in(i + 2)
            emit_conv_out(i)

    def wget(self, key):
        if self.ws_dry:
            self.ws_plan.append(key)
            return None, None
        n = 6
        while self.ws_issued < min(len(self.ws_plan), self.ws_consumed + n - 1):
            k = self.ws_issued
            s = k % n
            ap, trk, ncols = self.wsc[self.ws_plan[k]]
            self.fw.dma("sp", self.wring[s][:, 0:ncols], ap, self.s_w[s], reads=[trk], writes=[self.t_w[s]])
            self.ws_issued += 1
        k = self.ws_consumed
        assert self.ws_plan[k] == key, (self.ws_plan[k], key)
        self.ws_consumed += 1
        return self.wring[k % n], self.t_w[k % n]

    def main(self):
        dry = self.ws_dry
        if not dry:
            self.xsched = []
            off = 0
            for s, S in enumerate(self.seqs):
                for ps in (1, 2):
                    for g in range(S // GT):
                        for i in range(4):
                            r0 = off + g * GT + i * 128
                            if ps == 1:
                                self.xsched.append((self.x[r0:r0 + 128, :], None))
                            else:
                                self.xsched.append((self.h1s[r0:r0 + 128, :], self.t_h1[r0 // 128]))
                off += S
            self.x_issued = 0
            self.gpos = 0
        off = 0
        for s, S in enumerate(self.seqs):
            self.memkv(s)
            for g in range(S // GT):
                self.pass1(s, off, S, g)
            for g in range(S // GT):
                self.pass2(s, off, S, g)
            off += S

    def xtiles(self):
        base = self.gpos * 4
        upto = min(len(self.xsched), base + 6)
        while self.x_issued < upto:
            k = self.x_issued
            src, rt = self.xsched[k]
            if rt is not None and rt.w is None:
                assert k >= base + 4
                break
            sl = k % 6
            self.fw.dma("sp", self.xring[:, sl, :], src, self.s_x[sl], reads=[] if rt is None else [rt],
                        writes=[self.t_x[sl]])
            self.x_issued += 1
        assert self.x_issued >= base + 4
        self.gpos += 1
        return [(self.xring[:, (base + i) % 6, :], self.t_x[(base + i) % 6]) for i in range(4)]

    def norm_nT(self, tiles):
        n = len(tiles)
        ss = self.stat[:, 0:n]
        rs = self.stat[:, 8:8 + n]
        self.memset("pool", ss, 0.0, [self.t_ss])
        for i, (ap, trk) in enumerate(tiles):
            self.act(self.junk[:], ap, AF.Square, [trk], [self.t_junk, self.t_ss], accum=self.stat[:, i:i + 1])
        self.act(rs, ss, AF.Sqrt, [self.t_ss], [self.t_rs], scale=1.0 / D, bias=EPS)
        self.recip(rs, rs, [self.t_rs], [self.t_rs])
        for i, (ap, trk) in enumerate(tiles):
            if i % 2 == 0:
                self.ts("dve", self.xnb[:, i, :], ap, self.stat[:, 8 + i:9 + i], ALU.mult, [trk, self.t_rs], [self.t_xnb[i]])
            else:
                self.act(self.xnb[:, i, :], ap, AF.Copy, [trk, self.t_rs], [self.t_xnb[i]], scale=self.stat[:, 8 + i:9 + i])
        for kcp in range(4):
            pt, tpt = self.bankT()
            for i in range(n):
                for kk in range(2):
                    kc = kcp * 2 + kk
                    self.tr(pt[:, kk * GT + i * 128: kk * GT + (i + 1) * 128], self.xnb[:, i, kc * 128:(kc + 1) * 128],
                            [self.t_xnb[i]], tpt)
            src = pt[:].rearrange("p (k t) -> p k t", k=2)[:, :, 0:n * 128]
            self.cp("dve" if kcp % 2 == 0 else "act", self.nT[:, 2 * kcp:2 * kcp + 2, 0:n * 128], src, [tpt], [self.t_nT])

    def ffn(self, which, tiles):
        dry = self.ws_dry
        for j in range(NJ):
            w, wt = self.wget(("w1", which, j))
            if dry:
                continue
            wv = w[:, 0:2048].rearrange("p (k c) -> p k c", k=KC)
            pg, tg = self.bank()
            pu, tu = self.bank()
            for kc in range(KC):
                self.mm(pg[:], wv[:, kc, 0:128], self.nT[:, kc, :], kc == 0, kc == KC - 1, [wt, self.t_nT], tg)
            for kc in range(KC):
                self.mm(pu[:], wv[:, kc, 128:256], self.nT[:, kc, :], kc == 0, kc == KC - 1, [wt, self.t_nT], tu)
            tb, ttb = self.nextF()
            self.act(tb[:], pg[:], AF.Tanh, [tg], [ttb], scale=0.5)
            wb, twb = self.nextF()
            self.stt(wb[:], tb[:], 1.0, pg[:], ALU.add, ALU.mult, [ttb, tg], [twb])
            self.tt("dve", self.big[:, j, :], wb[:], pu[:], ALU.mult, [twb, tu], [self.t_big[j]])
        for c in range(2):
            accs = None if dry else [self.bank() for _ in range(4)]
            for jb in range(6):
                nj = 4 if jb < 5 else 2
                w, wt = self.wget(("w2", which, c, jb))
                if dry:
                    continue
                wv = w[:, 0:nj * 512].rearrange("p (j c) -> p j c", j=nj)
                for jj in range(nj):
                    j = jb * 4 + jj
                    for i in range(4):
                        self.mm(accs[i][0][:], self.big[:, j, i * 128:(i + 1) * 128], wv[:, jj, :], j == 0, j == NJ - 1,
                                [wt, self.t_big[j]], accs[i][1])
            if dry:
                continue
            for i in range(4):
                ap, trk = tiles[i]
                self.stt(ap[:, c * 512:(c + 1) * 512], accs[i][0][:], 0.25, ap[:, c * 512:(c + 1) * 512], ALU.mult, ALU.add,
                         [accs[i][1], trk], [trk])

    def load_cs(self, g):
        i = self.ics % 2
        self.ics += 1
        self.fw.dma("sp", self.csg[i][:], self.c_cs[:, 4 * g:4 * g + 4, :], self.s_csg[i], writes=[self.t_csg[i]])
        self.cs = self.csg[i]
        self.t_cs = self.t_csg[i]

    def headnorm_rope(self, ps, tps, H, gain, blk):
        W = H * 64
        sq, tsq = self.nextF()
        self.act(sq[:, 0:W], ps[:, 0:W], AF.Square, [tps], [tsq])
        ssq = self.stat[:, 16:16 + H]
        rsq = self.stat[:, 24:24 + H]
        self.fw.emit("dve", lambda h: h.reduce_sum(out=ssq, in_=sq[:, 0:W].rearrange("p (h d) -> p h d", h=H), axis=AX.X),
                     [tsq], [self.t_ssq])
        self.act(rsq, ssq, AF.Sqrt, [self.t_ssq], [self.t_rsq], scale=1.0 / 64, bias=EPS)
        self.recip(rsq, rsq, [self.t_rsq], [self.t_rsq])
        qn, tqn = self.nextF()
        qn3 = qn[:, 0:W].rearrange("p (h d) -> p h d", h=H)
        self.tt("dve", qn3, ps[:, 0:W].rearrange("p (h d) -> p h d", h=H), rsq.unsqueeze(2).to_broadcast([128, H, 64]),
                ALU.mult, [tps, self.t_rsq], [tqn])
        self.tt("dve", qn3, qn3, gain[:].unsqueeze(1).to_broadcast([128, H, 64]), ALU.mult, [tqn, self.t_gqk], [tqn])
        v = qn[:, 0:W].rearrange("p (h i two) -> p h i two", h=H, two=2)
        x0, x1 = v[:, :, :, 0], v[:, :, :, 1]
        o = self.q16[:, 0:W].rearrange("p (h i two) -> p h i two", h=H, two=2)
        cosb = self.cs[:, blk, 0:32].unsqueeze(1).to_broadcast([128, H, 32])
        sinb = self.cs[:, blk, 32:64].unsqueeze(1).to_broadcast([128, H, 32])
        ra, tra = self.nextF()
        rb, trb = self.nextF()
        ra3 = ra[:, 0:H * 32].rearrange("p (h i) -> p h i", h=H)
        ra3b = ra[:, 256:256 + H * 32].rearrange("p (h i) -> p h i", h=H)
        rb3 = rb[:, 0:H * 32].rearrange("p (h i) -> p h i", h=H)
        rb3b = rb[:, 256:256 + H * 32].rearrange("p (h i) -> p h i", h=H)
        rd = [tqn, self.t_cs]
        self.tt("dve", ra3, x0, cosb, ALU.mult, rd, [tra])
        self.tt("dve", ra3b, x1, sinb, ALU.mult, rd, [tra])
        self.tt("dve", o[:, :, :, 0], ra3, ra3b, ALU.subtract, [tra], [self.t_q16])
        self.tt("dve", rb3, x0, sinb, ALU.mult, rd, [trb])
        self.tt("dve", rb3b, x1, cosb, ALU.mult, rd, [trb])
        self.tt("dve", o[:, :, :, 1], rb3, rb3b, ALU.add, [trb], [self.t_q16])

    def memkv(self, s):
        dry = self.ws_dry
        if not dry:
            tiles = []
            for i in range(2):
                tmem = [self.t_fsc[4], self.t_fsc[5]]
                self.fw.dma("sp", self.memt, self.mem[s * MEM + i * 128: s * MEM + (i + 1) * 128, :], self.s_memt,
                            writes=tmem)
                ssn = self.stat[:, 32 + i:33 + i]
                self.memset("pool", ssn, 0.0, [self.t_ss])
                self.act(self.junk[:], self.memt, AF.Square, tmem, [self.t_junk, self.t_ss], accum=ssn)
                rsn = self.stat[:, 40 + i:41 + i]
                self.act(rsn, ssn, AF.Sqrt, [self.t_ss], [self.t_rs], scale=1.0 / D, bias=EPS)
                self.recip(rsn, rsn, [self.t_rs], [self.t_rs])
                self.ts("dve", self.xnb[:, i, :], self.memt, rsn, ALU.mult, tmem + [self.t_rs], [self.t_xnb[i]])
            for kcp in range(4):
                pt, tpt = self.bankT()
                for i in range(2):
                    for kk in range(2):
                        kc = kcp * 2 + kk
                        self.tr(pt[:, kk * GT + i * 128: kk * GT + (i + 1) * 128], self.xnb[:, i, kc * 128:(kc + 1) * 128],
                                [self.t_xnb[i]], tpt)
                src = pt[:].rearrange("p (k t) -> p k t", k=2)[:, :, 0:256]
                self.cp("dve" if kcp % 2 == 0 else "act", self.nT[:, 2 * kcp:2 * kcp + 2, 0:256], src, [tpt], [self.t_nT])
        for b in range(4):
            w, wt = self.wget(("wmem", b))
            if dry:
                continue
            wv = w[:, 0:2048].rearrange("p (k c) -> p k c", k=KC)
            if b < 2:
                for hh in range(2):
                    hc = b * 2 + hh
                    ps, tps = self.bank()
                    for kc in range(KC):
                        self.mm(ps[:, 0:256], wv[:, kc, hh * 128:(hh + 1) * 128], self.nT[:, kc, 0:256], kc == 0, kc == KC - 1,
                                [wt, self.t_nT], tps)
                    self.cp("act", self.KCT[:, hc, :], ps[:, 0:256], [tps], [self.t_KC])
            else:
                for mbk in range(2):
                    ps, tps = self.bank()
                    for kc in range(KC):
                        self.mm(ps[:, 0:256], self.nT[:, kc, mbk * 128:(mbk + 1) * 128], wv[:, kc, :], kc == 0, kc == KC - 1,
                                [wt, self.t_nT], tps)
                    self.cp("dve", self.VC[:, mbk, (b - 2) * 256:(b - 1) * 256], ps[:, 0:256], [tps], [self.t_VC])

    def pass1(self, s, off, S, g):
        dry = self.ws_dry
        tiles = None
        if not dry:
            tiles = self.xtiles()
            self.load_cs(g)
            self.norm_nT(tiles)
        self.ffn(0, tiles)
        if not dry:
            for i in range(4):
                r0 = off + g * GT + i * 128
                self.fw.dma("sp", self.h1s[r0:r0 + 128, :], tiles[i][0], self.s_x[(self.gpos * 4 - 4 + i) % 6],
                            reads=[tiles[i][1]], writes=[self.t_h1[r0 // 128]])
            self.norm_nT(tiles)
        w0, wt0 = self.wget(("wkv", 0))
        w1, wt1 = self.wget(("wkv", 1))
        if dry:
            return
        gg = (off + g * GT) // GT
        wv0 = w0[:, 0:2048].rearrange("p (k c) -> p k c", k=KC)
        wv1 = w1[:, 0:2048].rearrange("p (k c) -> p k c", k=KC)
        for i in range(4):
            blk = g * 4 + i
            ps, tps = self.bank()
            for kc in range(KC):
                self.mm(ps[:, 0:256], self.nT[:, kc, i * 128:(i + 1) * 128], wv0[:, kc, :], kc == 0, kc == KC - 1,
                        [wt0, self.t_nT], tps)
            self.cp("act", self.VA[:, blk, :, 0:64], ps[:, 128:256].rearrange("p (g d) -> p g d", g=2), [tps], [self.t_VA[g]])
            self.headnorm_rope(ps, tps, 2, self.gk, i)
            pt, tpt = self.bankT()
            self.tr(pt[:, 0:128], self.q16[:, 0:128], [self.t_q16], tpt)
            self.cp("act", self.KAT[:, blk * 128:(blk + 1) * 128], pt[:, 0:128], [tpt], [self.t_KA[g]])
            ps2, tps2 = self.bank()
            for kc in range(KC):
                self.mm(ps2[:, 0:128], self.nT[:, kc, i * 128:(i + 1) * 128], wv1[:, kc, 128:256], kc == 0, kc == KC - 1,
                        [wt1, self.t_nT], tps2)
            self.cp("dve", self.vbst[:, i, :, 0:64], ps2[:, 0:128].rearrange("p (g d) -> p g d", g=2), [tps2], [self.t_vbst])
        ps3, tps3 = self.bank()
        for kc in range(KC):
            self.mm(ps3[:], wv1[:, kc, 0:128], self.nT[:, kc, :], kc == 0, kc == KC - 1, [wt1, self.t_nT], tps3)
        self.cp("act", self.kbst[:], ps3[:], [tps3], [self.t_kbst])
        t0 = off + g * GT
        self.fw.dma("sp", self.kbs[:, t0:t0 + GT], self.kbst[:], self.s_kbst, reads=[self.t_kbst], writes=[self.t_kbs[gg]])
        self.fw.dma("sp", self.vbs[t0 // 128: t0 // 128 + 4].rearrange("b p c -> p b c"),
                    self.vbst[:].rearrange("p b g d -> p b (g d)"), self.s_vbst, reads=[self.t_vbst], writes=[self.t_vbs[gg]])

    def pass2(self, s, off, S, g):
        dry = self.ws_dry
        nblk = S // 128
        ngrp = S // GT
        tiles = None
        big = self.big
        tb = self.t_big
        QA, QB, QC, YA, YB, YC = 0, 4, 8, 12, 16, 20
        if not dry:
            tiles = self.xtiles()
            self.load_cs(g)
            wi = g % 2
            lo = max(0, 4 * g - 1)
            hi = min(nblk, 4 * g + 5)
            c0 = (lo - (4 * g - 1)) * 128
            gg0 = off // GT
            rds_k = [self.t_kbs[gg0 + x] for x in range(max(0, g - 1), min(ngrp, g + 2))]
            rds_v = [self.t_vbs[gg0 + x] for x in range(max(0, g - 1), min(ngrp, g + 2))]
            self.fw.dma("sp", self.KBw[wi][:, c0:c0 + (hi - lo) * 128], self.kbs[:, off + lo * 128: off + hi * 128],
                        self.s_KBw[wi], reads=rds_k, writes=[self.t_KBw[wi]])
            self.fw.dma("sp", self.VBw[wi][:, c0 // 128: c0 // 128 + (hi - lo), :],
                        self.vbs[off // 128 + lo: off // 128 + hi].rearrange("b p c -> p b c"),
                        self.s_VBw[wi], reads=rds_v, writes=[self.t_VBw[wi]])
            self.norm_nT(tiles)
        qa_ps = None if dry else [self.bank() for _ in range(4)]
        ub = None if dry else [self.bank(), self.bank()]
        for b in range(2):
            w, wt = self.wget(("wqa", b))
            if dry:
                continue
            wv = w[:, 0:2048].rearrange("p (k c) -> p k c", k=KC)
            for i in range(4):
                ps, tps = qa_ps[i]
                for kc in range(KC):
                    self.mm(ps[:, b * 256:(b + 1) * 256], self.nT[:, kc, i * 128:(i + 1) * 128], wv[:, kc, :],
                            kc == 0, kc == KC - 1, [wt, self.t_nT], tps)
        units = [("wqb", QB, "dve", 0), ("wqb", QB, "dve", 1), ("wqc", QC, "act", 0), ("wqc", QC, "act", 1)]
        for i in range(4):
            if not dry:
                ps, tps = qa_ps[i]
                self.headnorm_rope(ps, tps, 8, self.gq, i)
            nm, qbase, ceng, b = units[i]
            w, wt = self.wget((nm, b))
            if dry:
                continue
            wv = w[:, 0:2048].rearrange("p (k c) -> p k c", k=KC)
            for pp in range(2):
                p = 2 * b + pp
                ps2, tps2 = ub[pp]
                for kc in range(KC):
                    self.mm(ps2[:], wv[:, kc, pp * 128:(pp + 1) * 128], self.nT[:, kc, :], kc == 0, kc == KC - 1,
                            [wt, self.t_nT], tps2)
                self.cp(ceng, big[:, qbase + p, :], ps2[:], [tps2], [tb[qbase + p]])
            pt, tpt = self.bankT()
            for p in range(4):
                self.tr(pt[:, p * 128:(p + 1) * 128], self.q16[:, p * 128:(p + 1) * 128], [self.t_q16], tpt)
            self.cp("act", big[:, QA:QA + 4, i * 128:(i + 1) * 128], pt[:, 0:512].rearrange("p (a t) -> p a t", a=4),
                    [tpt], tb[QA:QA + 4])
        if not dry:
            kgr = lambda kb: kb // 4
            for p in range(4):
                acc = [self.bank(), self.bank()]
                sb = [self.bank() for _ in range(4)]
                pts = [None] * 4

                def qk(kb):
                    for gq in range(2):
                        sbk, tsbk = sb[(2 * kb + gq) % 4]
                        self.mm(sbk[:], self.KAT[64 * gq:64 * gq + 64, kb * 128:(kb + 1) * 128],
                                big[64 * gq:64 * gq + 64, QA + p, :], True, True, [self.t_KA[kgr(kb)], tb[QA + p]], tsbk)
                qk(0)
                for kb in range(nblk):
                    if kb + 1 < nblk:
                        qk(kb + 1)
                    for gq in range(2):
                        sbk, tsbk = sb[(2 * kb + gq) % 4]
                        ptile, tpt_ = self.nextPT()
                        pts[(2 * kb + gq) % 4] = (ptile, tpt_)
                        self.act(ptile[:], sbk[:], AF.Exp, [tsbk], [tpt_], scale=0.125)
                    for gq in range(2):
                        ptile, tpt_ = pts[(2 * kb + gq) % 4]
                        self.mm(acc[gq][0][:], self.VA[:, kb, gq, :], ptile[:], kb == 0, kb == nblk - 1,
                                [self.t_VA[kgr(kb)], tpt_], acc[gq][1])
                for gq in range(2):
                    rec, trec = self.nextF()
                    self.recip(rec[0:64, :], acc[gq][0][64:128, :], [acc[gq][1]], [trec])
                    self.tt("dve", big[64 * gq:64 * gq + 64, YA + p, :], acc[gq][0][0:64, :], rec[0:64, :], ALU.mult,
                            [acc[gq][1], trec], [tb[YA + p]])
            wi = g % 2
            KBw, VBw = self.KBw[wi], self.VBw[wi]
            for i in range(4):
                qb_ = 4 * g + i
                olist = [o for o in range(3) if 0 <= qb_ + o - 1 < nblk]
                accs = [(self.psT[k][:].bitcast(F32), self.t_psT[k]) for k in range(2)]
                steps = [(gq, idx, o) for gq in range(2) for idx, o in enumerate(olist)]
                sbs = [self.bank() for _ in steps]
                for n_, (gq, idx, o) in enumerate(steps):
                    wblk = i + o
                    self.mm(sbs[n_][0][:], KBw[64 * gq:64 * gq + 64, wblk * 128:(wblk + 1) * 128],
                            big[64 * gq:64 * gq + 64, QB:QB + 4, i * 128:(i + 1) * 128], True, True,
                            [self.t_KBw[wi]] + tb[QB:QB + 4], sbs[n_][1])
                sfs = []
                for n_, (gq, idx, o) in enumerate(steps):
                    sf, tsf = self.nextF()
                    self.stt(sf[:], sbs[n_][0][:], 0.125, self.bias[:, o, 4 * gq:4 * gq + 4, :].rearrange("p h i -> p (h i)"),
                             ALU.mult, ALU.add, [sbs[n_][1], self.t_bias], [tsf])
                    sfs.append((sf, tsf))
                ptl = []
                for n_, (gq, idx, o) in enumerate(steps):
                    ptile, tpt_ = self.nextPT()
                    self.act(ptile[:], sfs[n_][0][:], AF.Exp, [sfs[n_][1]], [tpt_])
                    ptl.append((ptile, tpt_))
                for n_, (gq, idx, o) in enumerate(steps):
                    wblk = i + o
                    self.mm(accs[gq][0], VBw[:, wblk, gq * 128:(gq + 1) * 128], ptl[n_][0][:], idx == 0, idx == len(olist) - 1,
                            [self.t_VBw[wi], ptl[n_][1]], accs[gq][1])
                for gq in range(2):
                    acc, tacc = accs[gq]
                    den, tden = self.nextF()
                    self.tt("dve", den[64:128, :].rearrange("p (h i) -> p h i", h=4),
                            acc[64:128, :].rearrange("p (h i) -> p h i", h=4),
                            self.esk[64:128, 4 * gq:4 * gq + 4].unsqueeze(2).to_broadcast([64, 4, 128]), ALU.add,
                            [tacc, self.t_esk], [tden])
                    self.recip(den[0:64, :], den[64:128, :], [tden], [tden])
                    self.tt("dve", big[64 * gq:64 * gq + 64, YB:YB + 4, i * 128:(i + 1) * 128],
                            acc[0:64, :].rearrange("p (h i) -> p h i", h=4), den[0:64, :].rearrange("p (h i) -> p h i", h=4),
                            ALU.mult, [tacc, tden], tb[YB:YB + 4])
            for hp in range(2):
                steps = [(hc, mbk) for hc in (2 * hp, 2 * hp + 1) for mbk in range(2)]
                sbs = [self.bank() for _ in steps]
                accO = [self.bank(), self.bank()]
                accS = [(self.psT[k][:].bitcast(F32), self.t_psT[k]) for k in range(2)]
                for n_, (hc, mbk) in enumerate(steps):
                    self.mm(sbs[n_][0][:], self.KCT[:, hc, mbk * 128:(mbk + 1) * 128], big[:, QC + hc, :], True, True,
                            [self.t_KC, tb[QC + hc]], sbs[n_][1])
                ptl = []
                for n_, (hc, mbk) in enumerate(steps):
                    ptile, tpt_ = self.nextPT()
                    self.act(ptile[:], sbs[n_][0][:], AF.Exp, [sbs[n_][1]], [tpt_], scale=128 ** -0.5)
                    ptl.append((ptile, tpt_))
                for n_, (hc, mbk) in enumerate(steps):
                    k = hc % 2
                    self.mm(accO[k][0][:], self.VC[:, mbk, hc * 128:(hc + 1) * 128], ptl[n_][0][:], mbk == 0, mbk == 1,
                            [self.t_VC, ptl[n_][1]], accO[k][1])
                    self.mm(accS[k][0], self.ones[:], ptl[n_][0][:], mbk == 0, mbk == 1, [self.t_ones, ptl[n_][1]], accS[k][1])
                for k in range(2):
                    hc = 2 * hp + k
                    rec, trec = self.nextF()
                    self.recip(rec[:], accS[k][0], [accS[k][1]], [trec])
                    self.tt("dve", big[:, YC + hc, :], accO[k][0][:], rec[:], ALU.mult, [accO[k][1], trec], [tb[YC + hc]])
        if self.debug and not dry and g == 0 and s == 0:
            self.out_ops.append(self.fw.dma("sp", self.dbg_big, self.big[:].rearrange("p a t -> p (a t)"), self.fw.new_dma_sem(),
                                            reads=self.t_big))
        for f in range(8):
            macc = tmacc = None
            for bi in range(3):
                w, wt = self.wget(("wm", f, bi))
                if dry:
                    continue
                gv = w[:, 0:1024].rearrange("p (k c) -> p k c", k=KC)
                bv = w[:, 1024:1536].rearrange("p (k c) -> p k c", k=4)
                ybase = (YA, YB, YC)[bi]
                pz, tpz = self.bank()
                pg, tpg = self.bank()
                for p in range(4):
                    self.mm(pz[:], bv[:, p, :], big[:, ybase + p, :], p == 0, p == 3, [wt, tb[ybase + p]], tpz)
                for kc in range(KC):
                    self.mm(pg[:], gv[:, kc, :], self.nT[:, kc, :], kc == 0, kc == KC - 1, [wt, self.t_nT], tpg)
                sg, tsg = self.nextF()
                self.act(sg[:], pg[:], AF.Tanh, [tpg], [tsg], scale=0.5)
                if bi == 0:
                    macc, tmacc = self.nextF()
                    self.stt(macc[:], sg[:], 1.0, pz[:], ALU.add, ALU.mult, [tsg, tpz], [tmacc])
                else:
                    mt, tmt = self.nextF()
                    self.stt(mt[:], sg[:], 1.0, pz[:], ALU.add, ALU.mult, [tsg, tpz], [tmt])
                    if bi == 1:
                        self.tt("dve", macc[:], macc[:], mt[:], ALU.add, [tmacc, tmt], [tmacc])
                    else:
                        self.tt("dve", big[:, f, :], macc[:], mt[:], ALU.add, [tmacc, tmt], [tb[f]])
        for c in range(2):
            accs = None if dry else [self.bank() for _ in range(4)]
            for kh in range(2):
                w, wt = self.wget(("wo", c, kh))
                if dry:
                    continue
                wv = w[:, 0:2048].rearrange("p (k c) -> p k c", k=4)
                for kk in range(4):
                    kc = kh * 4 + kk
                    for i in range(4):
                        self.mm(accs[i][0][:], big[:, kc, i * 128:(i + 1) * 128], wv[:, kk, :], kc == 0, kc == KC - 1,
                                [wt, tb[kc]], accs[i][1])
            if dry:
                continue
            for i in range(4):
                ap, trk = tiles[i]
                self.stt(ap[:, c * 512:(c + 1) * 512], accs[i][0][:], 0.5, ap[:, c * 512:(c + 1) * 512], ALU.mult, ALU.add,
                         [accs[i][1], trk], [trk])
        if self.debug and not dry and g == 0 and s == 0:
            for i in range(4):
                self.out_ops.append(self.fw.dma("sp", self.dbg_h2[i * 128:(i + 1) * 128, :], tiles[i][0], self.fw.new_dma_sem(),
                                                reads=[tiles[i][1]]))
        if not dry:
            self.norm_nT(tiles)
        self.ffn(1, tiles)
        if dry:
            return
        ss = self.stat[:, 48:52]
        rs = self.stat[:, 56:60]
        self.memset("pool", ss, 0.0, [self.t_ss])
        for i, (ap, trk) in enumerate(tiles):
            self.act(self.junk[:], ap, AF.Square, [trk], [self.t_junk, self.t_ss], accum=self.stat[:, 48 + i:49 + i])
        self.act(rs, ss, AF.Sqrt, [self.t_ss], [self.t_rs], scale=1.0 / D, bias=EPS)
        self.recip(rs, rs, [self.t_rs], [self.t_rs])
        for i, (ap, trk) in enumerate(tiles):
            self.stt(ap, ap, self.stat[:, 56 + i:57 + i], self.gfin[:], ALU.mult, ALU.mult, [trk, self.t_rs, self.t_gfin], [trk])
            r0 = off + g * GT + i * 128
            op = self.fw.dma("sp", self.y[r0:r0 + 128, :], ap, self.s_x[(self.gpos * 4 - 4 + i) % 6], reads=[trk])
            self.out_ops.append(op)


_PROG_CACHE = {}


def _get_program(seqs, debug=False):
    key = tuple(seqs) + (debug,)
    if key not in _PROG_CACHE:
        b = Builder(seqs, debug)
        nc = b.build()
        _PROG_CACHE[key] = (nc, b)
    return _PROG_CACHE[key]


def run_cores(per_core_x, per_core_mem, weights, seqs, debug=False):
    nc, b = _get_program(seqs, debug)
    cs, oh, ident = _host_constants(b.maxblk)
    in_maps = []
    for xc, mc in zip(per_core_x, per_core_mem):
        m = {"x": np.ascontiguousarray(xc, dtype=np.float32), "mem": np.ascontiguousarray(mc, dtype=np.float32),
             "c_cs": cs, "c_oh": oh, "c_ident": ident}
        for k, v in weights.items():
            m[k] = np.ascontiguousarray(v, dtype=np.float32)
        in_maps.append(m)
    res = run_bass_kernel_spmd(nc, in_maps, core_ids=list(range(len(in_maps))))
    if debug:
        return res.results
    return [r["y"] for r in res.results]


def kernel(x_prompt, x_sample, mem_prompt, mem_sample, rel_bias, norm_ffn1, ffn1_w_in, ffn1_w_out, norm_mix, w_in,
           q_norm_a, k_norm_a, sink_b, norm_mem, w_mem_kv, w_br_a, w_br_b, w_br_c, w_out, norm_ffn2, ffn2_w_in,
           ffn2_w_out, norm_final):
    x_prompt = np.asarray(x_prompt)
    x_sample = np.asarray(x_sample)
    mem_prompt = np.asarray(mem_prompt)
    mem_sample = np.asarray(mem_sample)
    ncores = 8
    BP, SP, _ = x_prompt.shape
    BS, SS, _ = x_sample.shape
    per = BS // ncores
    seqs = [SP] + [SS] * per
    weights = dict(rel_bias=rel_bias, norm_ffn1=norm_ffn1, ffn1_w_in=ffn1_w_in, ffn1_w_out=ffn1_w_out, norm_mix=norm_mix,
                   w_in=w_in, q_norm_a=q_norm_a, k_norm_a=k_norm_a, sink_b=sink_b, norm_mem=norm_mem, w_mem_kv=w_mem_kv,
                   w_br_a=w_br_a, w_br_b=w_br_b, w_br_c=w_br_c, w_out=w_out, norm_ffn2=norm_ffn2, ffn2_w_in=ffn2_w_in,
                   ffn2_w_out=ffn2_w_out, norm_final=norm_final)
    weights = {k: np.asarray(v) for k, v in weights.items()}
    xs, ms = [], []
    for c in range(ncores):
        xs.append(np.concatenate([x_prompt[c]] + [x_sample[c * per + i] for i in range(per)], axis=0))
        ms.append(np.concatenate([mem_prompt[c]] + [mem_sample[c * per + i] for i in range(per)], axis=0))
    ys = run_cores(xs, ms, weights, seqs)
    y_prompt = np.stack([ys[c][0:SP] for c in range(ncores)], axis=0).astype(np.float32)
    y_sample = np.stack([ys[c][SP + i * SS: SP + (i + 1) * SS] for c in range(ncores) for i in range(per)], axis=0)
    return (y_prompt, y_sample.astype(np.float32))
```
